# Optimizing a Trainium2 kernel written in Bass

```python
import jax
import jax.numpy as jnp
from jax import lax
import numpy as np

D_MODEL = 4096
BATCH = 4
SEQ = 2048
DEPTH = 4

GRID_W = 64
CTX_LEN = 256
FOURIER_GROUPS = 4
FOURIER_GROUP_W = D_MODEL // 16
FOURIER_W = FOURIER_GROUPS * FOURIER_GROUP_W
CONV_W = D_MODEL // 4
CONV_K = 31
MLA_HEADS = D_MODEL // 256
Q_LORA = D_MODEL // 4
KV_LORA = D_MODEL // 8
QK_NOPE = 128
QK_ROPE = 64
V_DIM = 128
MLA_W = MLA_HEADS * V_DIM
ROPE_THETA = 10000.0
ROPE_FREQS = QK_ROPE // 4
ATTN_SCALE = (QK_NOPE + QK_ROPE) ** -0.5
Q_BLOCK = 128
N_BRANCH = 3
GATE_RANK = 512
MOD_RANK = 256
N_MOD = 6
D_FF = 4 * D_MODEL
EPS = 1e-6
OFF_C = FOURIER_W
OFF_Q = OFF_C + 2 * CONV_W
OFF_KV = OFF_Q + Q_LORA
D_IN = OFF_KV + KV_LORA + QK_ROPE

kernel_name = "hybrid_fourier_conformer_mla_dit"


def rms_norm(x, g):
    x32 = x.astype(jnp.float32)
    y = x32 * lax.rsqrt(jnp.mean(jnp.square(x32), axis=-1, keepdims=True) + EPS)
    return (y * g.astype(jnp.float32)).astype(x.dtype)


def layer_norm(x, g, b):
    x32 = x.astype(jnp.float32)
    xc = x32 - jnp.mean(x32, axis=-1, keepdims=True)
    y = xc * lax.rsqrt(jnp.mean(jnp.square(xc), axis=-1, keepdims=True) + EPS)
    return (y * g.astype(jnp.float32) + b.astype(jnp.float32)).astype(x.dtype)


def axial_rope_tables(n_tokens):
    n_rows = n_tokens // GRID_W
    rows = jnp.repeat(jnp.arange(n_rows, dtype=jnp.float32), GRID_W)
    cols = jnp.tile(jnp.arange(GRID_W, dtype=jnp.float32), n_rows)
    pos = jnp.stack([rows, cols], axis=-1)
    inv_freq = jnp.power(ROPE_THETA, -jnp.arange(ROPE_FREQS, dtype=jnp.float32) / ROPE_FREQS)
    ang = pos[:, :, None] * inv_freq
    return jnp.cos(ang), jnp.sin(ang)


def apply_rope(x, cos, sin):
    xs = x.reshape(x.shape[:-1] + (2, 2, ROPE_FREQS))
    x1, x2 = xs[..., 0, :], xs[..., 1, :]
    cos = cos.astype(x.dtype)
    sin = sin.astype(x.dtype)
    out = jnp.stack([x1 * cos - x2 * sin, x2 * cos + x1 * sin], axis=-2)
    return out.reshape(x.shape)


def adaln(cond, w_a, w_b, b):
    m = (jax.nn.silu(cond) @ w_a) @ w_b + b
    return jnp.split(m, N_MOD, axis=-1)


def modulate(x, g, shift, scale):
    return rms_norm(x, g) * (1 + scale) + shift


def fourier_mix(u):
    b, l, _ = u.shape
    ug = u.astype(jnp.float32).reshape(b, l, FOURIER_GROUPS, FOURIER_GROUP_W)
    f = jnp.fft.fft2(ug, axes=(1, 3), norm="ortho").real
    return f.reshape(b, l, FOURIER_W).astype(u.dtype)


def conformer_conv(u, conv_w, conv_b, ln_g, ln_b):
    a, gt = jnp.split(u, 2, axis=-1)
    v = a * jax.nn.sigmoid(gt)
    y = lax.conv_general_dilated(
        v, conv_w[:, None, :].astype(v.dtype), window_strides=(1,),
        padding=((CONV_K // 2, CONV_K // 2),), dimension_numbers=("NWC", "WIO", "NWC"),
        feature_group_count=CONV_W)
    return jax.nn.silu(layer_norm(y + conv_b, ln_g, ln_b))


def mla_queries(cq, q_norm_g, w_uq, rope):
    b, l, _ = cq.shape
    q = (rms_norm(cq, q_norm_g) @ w_uq).reshape(b, l, MLA_HEADS, QK_NOPE + QK_ROPE)
    qn, qr = q[..., :QK_NOPE], q[..., QK_NOPE:]
    if rope is not None:
        qr = apply_rope(qr, rope[0][:, None], rope[1][:, None])
    return qn, qr


def mla_keys(ckv, kv_norm_g, w_ukv, rope):
    b, l, _ = ckv.shape
    c_kv, kr = ckv[..., :KV_LORA], ckv[..., KV_LORA:]
    kv = (rms_norm(c_kv, kv_norm_g) @ w_ukv).reshape(b, l, MLA_HEADS, QK_NOPE + V_DIM)
    kn, v = kv[..., :QK_NOPE], kv[..., QK_NOPE:]
    if rope is not None:
        kr = apply_rope(kr, rope[0], rope[1])
    return kn, kr, v


def mla_attend(qn, qr, kn, kr, v):
    s = (jnp.einsum("bqhd,bkhd->bhqk", qn, kn, preferred_element_type=jnp.float32)
         + jnp.einsum("bqhr,bkr->bhqk", qr, kr, preferred_element_type=jnp.float32))
    p = jax.nn.softmax(s * ATTN_SCALE, axis=-1).astype(v.dtype)
    return jnp.einsum("bhqk,bkhd->bqhd", p, v)


def mla_attend_blocked(qn, qr, kn, kr, v):
    b, l, h, _ = qn.shape
    nb = l // Q_BLOCK

    def to_blocks(t):
        return jnp.moveaxis(t.reshape((b, nb, Q_BLOCK) + t.shape[2:]), 1, 0)

    o = lax.map(lambda q: mla_attend(q[0], q[1], kn, kr, v), (to_blocks(qn), to_blocks(qr)))
    return jnp.moveaxis(o, 0, 1).reshape(b, l, h, V_DIM)


def token_mixer(h, u, own_kv, ctx_kv, rope, lp):
    b, l, _ = h.shape
    y_f = fourier_mix(u[..., :OFF_C]) @ lp["w_pf"]
    y_c = conformer_conv(u[..., OFF_C:OFF_Q], lp["conv_w"], lp["conv_b"],
                         lp["conv_ln_g"], lp["conv_ln_b"]) @ lp["w_pc"]
    qn, qr = mla_queries(u[..., OFF_Q:OFF_KV], lp["q_norm_g"], lp["w_uq"], rope)
    kn, kr, v = own_kv
    if ctx_kv is None:
        o = mla_attend(qn, qr, kn, kr, v)
    else:
        kn = jnp.concatenate([kn, ctx_kv[0]], axis=1)
        kr = jnp.concatenate([kr, ctx_kv[1]], axis=1)
        v = jnp.concatenate([v, ctx_kv[2]], axis=1)
        o = mla_attend_blocked(qn, qr, kn, kr, v)
    y_m = o.reshape(b, l, MLA_W) @ lp["w_pm"]
    gates = jax.nn.sigmoid((h @ lp["w_gate_a"]) @ lp["w_gate_b"] + lp["b_gate"])
    g_f, g_c, g_m = jnp.split(gates, N_BRANCH, axis=-1)
    return (g_f * y_f + g_c * y_c + g_m * y_m) @ lp["w_out"]


def sq_relu_mlp(h, w1, w2):
    return jnp.square(jax.nn.relu(h @ w1)) @ w2


def setup_inputs(seed: int = 0) -> dict:
    key = jax.random.key(seed)
    ks = list(jax.random.split(key, 32))

    def nrm(k, shape, scale):
        return jax.random.normal(k, shape, jnp.float32) * scale

    def gain(k, shape):
        return 1.0 + 0.05 * jax.random.normal(k, shape, jnp.float32)

    L = DEPTH
    return {
        "x": nrm(ks[0], (BATCH, SEQ, D_MODEL), 1.0),
        "c": nrm(ks[1], (BATCH, D_MODEL), 1.0),
        "ctx": nrm(ks[2], (BATCH, CTX_LEN, D_MODEL), 1.0),
        "c_ctx": nrm(ks[3], (D_MODEL,), 1.0),
        "g_mix_pre": gain(ks[4], (L, D_MODEL)),
        "g_mix_post": gain(ks[5], (L, D_MODEL)),
        "g_mlp_pre": gain(ks[6], (L, D_MODEL)),
        "g_mlp_post": gain(ks[7], (L, D_MODEL)),
        "w_mod_a": nrm(ks[8], (L, D_MODEL, MOD_RANK), D_MODEL ** -0.5),
        "w_mod_b": nrm(ks[9], (L, MOD_RANK, N_MOD * D_MODEL), 0.5 * MOD_RANK ** -0.5),
        "b_mod": nrm(ks[10], (L, N_MOD * D_MODEL), 0.02),
        "w_in": nrm(ks[11], (L, D_MODEL, D_IN), D_MODEL ** -0.5),
        "conv_w": nrm(ks[12], (L, CONV_K, CONV_W), CONV_K ** -0.5),
        "conv_b": nrm(ks[13], (L, CONV_W), 0.02),
        "conv_ln_g": gain(ks[14], (L, CONV_W)),
        "conv_ln_b": nrm(ks[15], (L, CONV_W), 0.02),
        "q_norm_g": gain(ks[16], (L, Q_LORA)),
        "w_uq": nrm(ks[17], (L, Q_LORA, MLA_HEADS * (QK_NOPE + QK_ROPE)), Q_LORA ** -0.5),
        "kv_norm_g": gain(ks[18], (L, KV_LORA)),
        "w_ukv": nrm(ks[19], (L, KV_LORA, MLA_HEADS * (QK_NOPE + V_DIM)), KV_LORA ** -0.5),
        "w_pf": nrm(ks[20], (L, FOURIER_W, D_MODEL), FOURIER_W ** -0.5),
        "w_pc": nrm(ks[21], (L, CONV_W, D_MODEL), CONV_W ** -0.5),
        "w_pm": nrm(ks[22], (L, MLA_W, D_MODEL), MLA_W ** -0.5),
        "w_gate_a": nrm(ks[23], (L, D_MODEL, GATE_RANK), D_MODEL ** -0.5),
        "w_gate_b": nrm(ks[24], (L, GATE_RANK, N_BRANCH * D_MODEL), GATE_RANK ** -0.5),
        "b_gate": nrm(ks[25], (L, N_BRANCH * D_MODEL), 0.02),
        "w_out": nrm(ks[26], (L, D_MODEL, D_MODEL), D_MODEL ** -0.5),
        "w_ff1": nrm(ks[27], (L, D_MODEL, D_FF), D_MODEL ** -0.5),
        "w_ff2": nrm(ks[28], (L, D_FF, D_MODEL), D_FF ** -0.5),
    }


def reference(x, c, ctx, c_ctx, g_mix_pre, g_mix_post, g_mlp_pre, g_mlp_post,
              w_mod_a, w_mod_b, b_mod, w_in, conv_w, conv_b, conv_ln_g, conv_ln_b,
              q_norm_g, w_uq, kv_norm_g, w_ukv, w_pf, w_pc, w_pm,
              w_gate_a, w_gate_b, b_gate, w_out, w_ff1, w_ff2):
    rope = axial_rope_tables(x.shape[1])
    t = ctx
    for i in range(DEPTH):
        last = i == DEPTH - 1
        lp = {
            "conv_w": conv_w[i], "conv_b": conv_b[i], "conv_ln_g": conv_ln_g[i],
            "conv_ln_b": conv_ln_b[i], "q_norm_g": q_norm_g[i], "w_uq": w_uq[i],
            "w_pf": w_pf[i], "w_pc": w_pc[i], "w_pm": w_pm[i],
            "w_gate_a": w_gate_a[i], "w_gate_b": w_gate_b[i], "b_gate": b_gate[i],
            "w_out": w_out[i],
        }
        sx_a, cx_a, gx_a, sx_m, cx_m, gx_m = adaln(c[:, None, :], w_mod_a[i], w_mod_b[i], b_mod[i])
        st_a, ct_a, gt_a, st_m, ct_m, gt_m = adaln(c_ctx[None, None, :], w_mod_a[i], w_mod_b[i], b_mod[i])

        ht = modulate(t, g_mix_pre[i], st_a, ct_a)
        ut = ht @ w_in[i]
        kv_t = mla_keys(ut[..., OFF_KV:], kv_norm_g[i], w_ukv[i], None)
        hx = modulate(x, g_mix_pre[i], sx_a, cx_a)
        ux = hx @ w_in[i]
        kv_x = mla_keys(ux[..., OFF_KV:], kv_norm_g[i], w_ukv[i], rope)
        yx = token_mixer(hx, ux, kv_x, kv_t, rope, lp)
        x = x + gx_a * rms_norm(yx, g_mix_post[i])

        hx = modulate(x, g_mlp_pre[i], sx_m, cx_m)
        x = x + gx_m * rms_norm(sq_relu_mlp(hx, w_ff1[i], w_ff2[i]), g_mlp_post[i])

        if not last:
            yt = token_mixer(ht, ut, kv_t, None, None, lp)
            t = t + gt_a * rms_norm(yt, g_mix_post[i])
            ht = modulate(t, g_mlp_pre[i], st_m, ct_m)
            t = t + gt_m * rms_norm(sq_relu_mlp(ht, w_ff1[i], w_ff2[i]), g_mlp_post[i])
    return x
```

```python
import numpy as np
from contextlib import ExitStack
import concourse.bass as bass
import concourse.mybir as mybir
from concourse.bass_utils import run_bass_kernel_spmd

F32 = mybir.dt.float32
BF16 = mybir.dt.bfloat16
ALU = mybir.AluOpType
AF = mybir.ActivationFunctionType

SEM_LIMIT = 30000
import os
DBG_NOROPE = bool(os.environ.get('NOROPE'))
EPS = 1e-6
CONV_K = 31
GRID_W = 64
ROPE_THETA = 10000.0


class Cfg:
    def __init__(s, D=4096, S=2048, CTX=256, DEPTH=4):
        s.D, s.S, s.CTX, s.DEPTH = D, S, CTX, DEPTH
        s.T = S + CTX
        s.DC = D // 128
        s.FGW = D // 16
        s.FW = 4 * s.FGW
        s.CW = D // 4
        s.H = D // 256
        s.QL = D // 4
        s.KVL = D // 8
        s.MW = s.H * 128
        s.GR = 512
        s.MR = 256
        s.DFF = 4 * D
        s.OFF_C = s.FW
        s.OFF_Q = s.OFF_C + 2 * s.CW
        s.OFF_KV = s.OFF_Q + s.QL
        s.DIN = s.OFF_KV + s.KVL + 64
        s.DINP = ((s.DIN + 127) // 128) * 128
        s.KG = 32
        s.TB = 512
        s.NTT = s.T // 128
        s.ATTN_SCALE = float((128 + 64) ** -0.5)
        o = 0
        s.V = {}
        for nm, n in (("g_mix_pre", s.DC), ("g_mix_post", s.DC), ("g_mlp_pre", s.DC), ("g_mlp_post", s.DC),
                      ("b_mod", 6 * s.DC), ("conv_b", s.CW // 128), ("conv_ln_g", s.CW // 128),
                      ("conv_ln_b", s.CW // 128), ("q_norm_g", s.QL // 128), ("kv_norm_g", s.KVL // 128),
                      ("b_gate", 3 * s.DC), ("conv_w", CONV_K * (s.CW // 128))):
            s.V[nm] = (o, n)
            o += n
        s.NV = o

    def tblocks(s, lat_only=False):
        bl = [(t0, min(s.TB, s.S - t0)) for t0 in range(0, s.S, s.TB)]
        if not lat_only:
            bl += [(s.S + t0, min(s.TB, s.CTX - t0)) for t0 in range(0, s.CTX, s.TB)]
        return bl


class Buf:
    __slots__ = ("name", "w_evs", "r_evs")

    def __init__(self, name=""):
        self.name = name
        self.w_evs = {}
        self.r_evs = {}


class TK:
    def __init__(self, nc):
        self.nc = nc
        self.stack = ExitStack()
        self.eng = {"pe": nc.tensor, "act": nc.scalar, "dve": nc.vector, "pool": nc.gpsimd, "sp": nc.sync}
        self.cur = {}
        self.waited = {e: {} for e in self.eng}
        self.nsem = 0
        self.dq = {}
        self.dq_i = {}
        self.last_ev = {}
        self.n_ins = 0
        for q, m in (("sp", 12), ("pool", 8), ("act", 8)):
            self.dq[q] = [[self.new_sem(), 0] for _ in range(m)]
            self.dq_i[q] = 0

    def new_sem(self):
        self.nsem += 1
        return self.stack.enter_context(self.nc.semaphore("s%d" % self.nsem))

    def mark(self, ins, e):
        c = self.cur.get(e)
        if c is None or c[1] >= SEM_LIMIT:
            c = [self.new_sem(), 0]
            self.cur[e] = c
        c[1] += 1
        ins.then_inc(c[0], 1)
        ev = (c[0], c[1], e)
        self.last_ev[id(c[0])] = ev
        return ev

    def wait(self, e, ev):
        sem, val = ev[0], ev[1]
        w = self.waited[e].get(id(sem))
        if w is not None and w >= val:
            return
        self.waited[e][id(sem)] = val
        self.eng[e].wait_ge(sem, val)
        self.n_ins += 1

    def _pre(self, e, reads, writes, own_acc=False):
        for b in reads:
            for ev in b.w_evs.values():
                self.wait(e, ev)
        for b in writes:
            for ev in b.w_evs.values():
                if own_acc and ev[2] == e:
                    continue
                self.wait(e, ev)
            for ev in b.r_evs.values():
                self.wait(e, ev)

    def _post(self, evs, reads, writes):
        for b in reads:
            for ev in evs:
                b.r_evs[id(ev[0])] = ev
        for b in writes:
            b.w_evs = {id(ev[0]): ev for ev in evs}
            b.r_evs = {}

    def op(self, e, fn, reads=(), writes=()):
        self._pre(e, reads, writes)
        ins = fn()
        self.n_ins += 1
        ev = self.mark(ins, e)
        self._post([ev], reads, writes)
        return ev

    def group(self, e, fns, reads=(), writes=(), own_acc=False):
        self._pre(e, reads, writes, own_acc)
        ins = None
        for fn in fns:
            ins = fn()
            self.n_ins += 1
        ev = self.mark(ins, e)
        self._post([ev], reads, writes)
        return ev

    def dma(self, q, pairs, reads=(), writes=()):
        self._pre(q, reads, writes)
        evs = []
        for out, in_ in pairs:
            lst = self.dq[q]
            i = self.dq_i[q]
            self.dq_i[q] = (i + 1) % len(lst)
            slot = lst[i]
            if slot[1] >= SEM_LIMIT:
                self.wait(q, (slot[0], slot[1]))
                slot[0] = self.new_sem()
                slot[1] = 0
            if slot[1] > 0:
                self.wait(q, (slot[0], slot[1]))
            ins = self.eng[q].dma_start(out=out, in_=in_)
            self.n_ins += 1
            slot[1] += 16
            ins.then_inc(slot[0], 16)
            ev = (slot[0], slot[1], "dma_" + q)
            self.last_ev[id(slot[0])] = ev
            evs.append(ev)
        self._post(evs, reads, writes)
        return evs

    def barrier(self):
        evs = list(self.last_ev.values())
        for e in self.eng:
            for ev in evs:
                self.wait(e, ev)


class Rot:
    def __init__(self, items):
        self.items = items
        self.i = 0

    def next(self):
        it = self.items[self.i % len(self.items)]
        self.i += 1
        return it


def build_program(cfg, dbg=(), stop=None):
    c = cfg
    nc = bass.Bass("TRN2", target_bir_lowering=False)
    L = c.DEPTH
    D, S, CTX, T, DC = c.D, c.S, c.CTX, c.T, c.DC

    def din(name, shape):
        return nc.dram_tensor(name, list(shape), F32, kind="ExternalInput").ap()

    xin = din("xin", [T, D])
    cond = din("cond", [2 * DC, 128])
    vecs = din("vecs", [L, c.NV, 128])
    w_mod_a = din("w_mod_a", [L, D, c.MR])
    w_mod_b = din("w_mod_b", [L, c.MR, 6 * D])
    w_in = din("w_in", [L, D, c.DIN])
    w_uq = din("w_uq", [L, c.QL, c.H * 192])
    w_ukv = din("w_ukv", [L, c.KVL, c.H * 256])
    w_pf = din("w_pf", [L, c.FW, D])
    w_pc = din("w_pc", [L, c.CW, D])
    w_pm = din("w_pm", [L, c.MW, D])
    w_gate_a = din("w_gate_a", [L, D, c.GR])
    w_gate_b = din("w_gate_b", [L, c.GR, 3 * D])
    w_out = din("w_out", [L, D, D])
    w_ff1 = din("w_ff1", [L, D, c.DFF])
    w_ff2 = din("w_ff2", [L, c.DFF, D])
    ident_in = din("ident", [128, 128])
    perm_in = din("perm", [128, 128])
    rope_in = din("rope", [2, 128, T])
    dftc_in = din("dftc", [c.FW, 2 * c.FW])
    dftp_in = din("dftp", [2 * S, S])
    dftx_in = din("dftx", [2 * CTX, CTX])
    y_out = nc.dram_tensor("y", [S, D], F32, kind="ExternalOutput").ap()

    def scratch(name, shape, dt=BF16):
        if name in dbg:
            return nc.dram_tensor(name, list(shape), dt, kind="ExternalOutput").ap()
        return nc.dram_tensor(name, list(shape), dt).ap()

    R = scratch("R", [T, D], F32)
    HT = scratch("HT", [D, T])
    UT = scratch("UT", [c.DINP + c.GR, T])
    VT = scratch("VT", [c.CW, T], F32)
    CQG = scratch("CQG", [c.QL, T])
    CKVG = scratch("CKVG", [c.KVL, T])
    KRT = scratch("KRT", [64, T])
    QT = scratch("QT", [c.H * 192, T])
    KNT = scratch("KNT", [c.H * 128, T])
    VV = scratch("VV", [T, c.H * 128])
    OT = scratch("OT", [c.MW, T])
    AB = scratch("AB", [T, 2 * c.FW])
    FT = scratch("FT", [c.FW, T])
    CT = scratch("CT", [c.CW, T])
    MT = scratch("MT", [D, T])
    Y = scratch("Y", [T, D], F32)
    HID = scratch("HID", [c.DFF, T])
    DFTP = scratch("DFTPb", [2 * S, S])
    DFTX = scratch("DFTXb", [2 * CTX, CTX])

    tk = TK(nc)
    nm = [0]

    def sbt(st, shape, dt, name=None):
        nm[0] += 1
        return st.enter_context(nc.sbuf_tensor("%s_%d" % (name or "t", nm[0]), list(shape), dt))

    def pst(st, shape, dt, name=None):
        nm[0] += 1
        return st.enter_context(nc.psum_tensor("%s_%d" % (name or "p", nm[0]), list(shape), dt))

    def mm(out, lhsT, rhs, start, stop):
        return lambda: nc.tensor.matmul(out, lhsT=lhsT, rhs=rhs, start=start, stop=stop)

    def fm(ap):
        return ap.rearrange("(kc p) t -> p kc t", p=128)

    def split_k(kcn, step=8):
        return [(k0, min(step, kcn - k0)) for k0 in range(0, kcn, step)]

    P = tk.stack
    idf = sbt(P, [128, 128], F32, "idf")
    idb = sbt(P, [128, 128], BF16, "idb")
    ones_f = sbt(P, [128, 128], F32, "ones_f")
    ones_b = sbt(P, [128, 128], BF16, "ones_b")
    permb = sbt(P, [128, 128], BF16, "permb")
    condT = sbt(P, [128, 2 * DC], BF16, "condT")
    vecT = sbt(P, [128, c.NV], F32, "vecT")
    modT = sbt(P, [128, 6, 2, DC], F32, "modT")
    gsa = sbt(P, [128, 2, DC], F32, "gsa")
    gsm = sbt(P, [128, 2, DC], F32, "gsm")
    gpa = sbt(P, [128, 2, DC], F32, "gpa")
    gpm = sbt(P, [128, 2, DC], F32, "gpm")
    rstd_q = sbt(P, [128, T], F32, "rstd_q")
    rstd_kv = sbt(P, [128, T], F32, "rstd_kv")
    rstd_kv_tm = sbt(P, [128, c.NTT], F32, "rstd_kv_tm")
    Bc = Buf("consts")
    Bvec = Buf("vecT")
    Bmod = Buf("mod")
    Brq = Buf("rstd_q")
    Brkv = Buf("rstd_kv")

    def vcol(name, j=0, n=1):
        o, _ = c.V[name]
        return vecT[:, o + j:o + j + n]

    def gemm(orient, kgroups, a_src, tblocks, nblocks, epi, st, banks, a_slots=2, w_slots=2):
        kmax = max(k for _, k in kgroups)
        wmax = max(nb["width"] for nb in nblocks)
        tmax = max(tw for _, tw in tblocks)
        wr = Rot([(sbt(st, [128, kmax, wmax], BF16, "W"), Buf("W")) for _ in range(w_slots)])
        ar = Rot([(sbt(st, [128, kmax, tmax], BF16, "A"), Buf("A")) for _ in range(a_slots)])
        for nbi, nb in enumerate(nblocks):
            for kgi, (kc0, kcn) in enumerate(kgroups):
                W, Wb = wr.next()
                tk.dma("pool", [(W[:, ko:ko + kn, co:co + cw], ap) for (ko, kn, co, cw, ap) in nb["load"](kc0, kcn)],
                       writes=[Wb])
                for tbi, (t0, tw) in enumerate(tblocks):
                    A, Ab = ar.next()
                    tk.dma("sp", [(A[:, ko:ko + kn, 0:tw], ap) for (ko, kn, ap) in a_src(kc0, kcn, t0, tw)],
                           writes=[Ab])
                    info = dict(nb=nbi, kg=kgi, nkg=len(kgroups), tb=tbi, t0=t0, tw=tw)
                    if orient == "FM":
                        for si, (off, m, tag) in enumerate(nb["subs"]):
                            pt, pb = banks.next()
                            tk.group("pe", [mm(pt[0:m, 0:tw], W[:, k, off:off + m], A[:, k, 0:tw], k == 0, k == kcn - 1)
                                            for k in range(kcn)], reads=[Wb, Ab], writes=[pb])
                            epi(pt, pb, m, tw, dict(info, si=si, tag=tag))
                    else:
                        for tt in range(tw // 128):
                            for si, (off, n, tag) in enumerate(nb["subs"]):
                                pt, pb = banks.next()
                                tk.group("pe", [mm(pt[:, 0:n], A[:, k, tt * 128:(tt + 1) * 128], W[:, k, off:off + n],
                                                   k == 0, k == kcn - 1) for k in range(kcn)],
                                         reads=[Wb, Ab], writes=[pb])
                                epi(pt, pb, 128, n, dict(info, si=si, tag=tag, tt=t0 // 128 + tt))

    def a_from(ATd, row0=0):
        v = fm(ATd)
        r0 = row0 // 128

        def f(kc0, kcn, t0, tw):
            return [(ko, kn, v[:, r0 + kc0 + ko:r0 + kc0 + ko + kn, t0:t0 + tw]) for ko, kn in split_k(kcn)]
        return f

    def w_cols(Wd, col0, width, row0=0):
        v = fm(Wd)
        r0 = row0 // 128

        def f(kc0, kcn):
            return [(ko, kn, 0, width, v[:, r0 + kc0 + ko:r0 + kc0 + ko + kn, col0:col0 + width])
                    for ko, kn in split_k(kcn)]
        return f

    def w_segs(Wd, segs):
        v = fm(Wd)

        def f(kc0, kcn):
            out = []
            co = 0
            for col0, width in segs:
                for ko, kn in split_k(kcn):
                    out.append((ko, kn, co, width, v[:, kc0 + ko:kc0 + ko + kn, col0:col0 + width]))
                co += width
            return out
        return f

    def simple_nblocks(Wd, col0, ncols, tagf, nbw=512, subw=128):
        nbl = []
        for b0 in range(0, ncols, nbw):
            w = min(nbw, ncols - b0)
            subs = [(o, min(subw, w - o), tagf(col0 + b0 + o)) for o in range(0, w, subw)]
            nbl.append(dict(width=w, load=w_cols(Wd, col0 + b0, w), subs=subs))
        return nbl

    def fp_banks(st, n, name="pb"):
        return Rot([(pst(st, [128, 512], F32, name), Buf(name)) for _ in range(n)])

    class Stager:
        def __init__(s, st, n, shape, dt):
            s.rot = Rot([(sbt(st, shape, dt, "stg"), Buf("stg")) for _ in range(n)])
            s.flip = 0

        def next(s):
            return s.rot.next()

    def evac_copy(eng, out_ap, in_ap, reads, writes):
        if eng == "act":
            return tk.op("act", lambda: nc.scalar.copy(out=out_ap, in_=in_ap), reads=reads, writes=writes)
        return tk.op("dve", lambda: nc.vector.tensor_copy(out=out_ap, in_=in_ap), reads=reads, writes=writes)

    flip = [0]

    def alt():
        flip[0] ^= 1
        return "act" if flip[0] else "dve"

    def rsqrt_ops(out_ap, in_ap, scale, reads, writes):
        tk.op("act", lambda: nc.scalar.activation(out=out_ap, in_=in_ap, func=AF.Sqrt, scale=scale, bias=EPS),
              reads=reads, writes=writes)
        tk.op("dve", lambda: nc.vector.reciprocal(out=out_ap, in_=out_ap), reads=writes, writes=writes)

    with ExitStack() as st:
        ps = pst(st, [128, 512], F32)
        pb = Buf()
        tmp = sbt(st, [128, 128], F32)
        tb_ = Buf()
        tk.dma("sp", [(idf[:], ident_in)], writes=[Bc])
        tk.dma("pool", [(permb[:], perm_in)], writes=[Bc])
        tk.op("dve", lambda: nc.vector.tensor_copy(out=idb[:], in_=idf[:]), reads=[Bc], writes=[Bc])
        tk.op("dve", lambda: nc.vector.memset(ones_f[:], 1.0), writes=[Bc])
        tk.op("dve", lambda: nc.vector.memset(ones_b[:], 1.0), writes=[Bc])
        tk.dma("sp", [(tmp[0:2 * DC, :], cond)], writes=[tb_])
        tk.op("pe", lambda: nc.tensor.transpose(ps[:, 0:2 * DC], tmp[0:2 * DC, :], idf[0:2 * DC, 0:2 * DC]),
              reads=[tb_, Bc], writes=[pb])
        tk.op("act", lambda: nc.scalar.activation(out=condT[:], in_=ps[:, 0:2 * DC], func=AF.Silu),
              reads=[pb], writes=[Bc])
        rows = 256
        tk.dma("sp", [(R[r0:r0 + rows, :], xin[r0:r0 + rows, :]) for r0 in range(0, T, rows)])
        tk.dma("pool", [(DFTP[r0:r0 + 128, :], dftp_in[r0:r0 + 128, :]) for r0 in range(0, 2 * S, 128)])
        tk.dma("pool", [(DFTX[r0:r0 + 128, :], dftx_in[r0:r0 + 128, :]) for r0 in range(0, 2 * CTX, 128)])
    tk.barrier()

    def phase_vectors(l):
        with ExitStack() as st:
            ps = pst(st, [128, 512], F32)
            pb = Buf()
            rin = Rot([(sbt(st, [128, 128], F32), Buf()) for _ in range(2)])
            for r0 in range(0, c.NV, 128):
                n = min(128, c.NV - r0)
                t_, b_ = rin.next()
                tk.dma("sp", [(t_[0:n, :], vecs[l, r0:r0 + n, :])], writes=[b_])
                tk.op("pe", lambda: nc.tensor.transpose(ps[:, 0:n], t_[0:n, :], idf[0:n, 0:n]),
                      reads=[b_, Bc], writes=[pb])
                tk.op("dve", lambda: nc.vector.tensor_copy(out=vecT[:, r0:r0 + n], in_=ps[:, 0:n]),
                      reads=[pb], writes=[Bvec])
            wa = sbt(st, [128, DC, c.MR], BF16)
            wab = Buf()
            tk.dma("pool", [(wa[:, ko:ko + kn, :], fm(w_mod_a[l])[:, ko:ko + kn, :]) for ko, kn in split_k(DC)],
                   writes=[wab])
            m1 = sbt(st, [128, 2, 2], BF16)
            m1b = Buf()
            condv = condT[:].rearrange("p (two j) -> p j two", two=2)
            for r in range(2):
                tk.group("pe", [mm(ps[:, 2 * r:2 * r + 2], wa[:, j, r * 128:(r + 1) * 128], condv[:, j, :], j == 0, j == DC - 1)
                                for j in range(DC)], reads=[wab, Bc], writes=[pb])
            tk.op("dve", lambda: nc.vector.tensor_copy(out=m1[:].rearrange("p a b -> p (a b)"), in_=ps[:, 0:4]),
                  reads=[pb], writes=[m1b])
            wbr = Rot([(sbt(st, [128, 2, D], BF16), Buf()) for _ in range(2)])
            bo, _ = c.V["b_mod"]
            for j6 in range(6):
                wb, wbb = wbr.next()
                tk.dma("pool", [(wb[:, :, :], fm(w_mod_b[l])[:, :, j6 * D:(j6 + 1) * D])], writes=[wbb])
                fns = []
                for cc in range(DC):
                    for r in range(2):
                        fns.append(mm(ps[:, 2 * cc:2 * cc + 2], wb[:, r, cc * 128:(cc + 1) * 128], m1[:, r, :], r == 0, r == 1))
                tk.group("pe", fns, reads=[wbb, m1b], writes=[pb])
                pv = ps[:, 0:2 * DC].rearrange("p (c two) -> p c two", two=2)
                for cd in range(2):
                    tk.op("dve", lambda cd=cd: nc.vector.tensor_tensor(
                        out=modT[:, j6, cd, :], in0=pv[:, :, cd], in1=vecT[:, bo + j6 * DC:bo + (j6 + 1) * DC], op=ALU.add),
                        reads=[pb, Bvec], writes=[Bmod])
            for cd in range(2):
                for (dst, sc_i, gpre) in ((gsa, 1, "g_mix_pre"), (gsm, 4, "g_mlp_pre")):
                    tk.op("dve", lambda dst=dst, sc_i=sc_i, gpre=gpre, cd=cd: nc.vector.scalar_tensor_tensor(
                        out=dst[:, cd, :], in0=modT[:, sc_i, cd, :], scalar=1.0, in1=vcol(gpre, 0, DC),
                        op0=ALU.add, op1=ALU.mult), reads=[Bmod, Bvec], writes=[Bmod])
                for (dst, g_i, gpost) in ((gpa, 2, "g_mix_post"), (gpm, 5, "g_mlp_post")):
                    tk.op("dve", lambda dst=dst, g_i=g_i, gpost=gpost, cd=cd: nc.vector.tensor_tensor(
                        out=dst[:, cd, :], in0=modT[:, g_i, cd, :], in1=vcol(gpost, 0, DC), op=ALU.mult),
                        reads=[Bmod, Bvec], writes=[Bmod])
        tk.barrier()

    def modulate_tiles(st, tiles_src, gs, sh_idx, lat_only):
        raise NotImplementedError

    def phase_modulate(l, gs, sh_idx, lat_only=False, resid=None):
        with ExitStack() as st:
            xr = Rot([(sbt(st, [128, D], F32, "x"), Buf()) for _ in range(2)])
            xnr = Rot([(sbt(st, [128, D], BF16, "xn"), Buf()) for _ in range(1)])
            sq = sbt(st, [128, D], BF16, "sq")
            sqb = Buf()
            ssr = Rot([(sbt(st, [128, 4], F32, "ss"), Buf()) for _ in range(2)])
            hr = Rot([(sbt(st, [128, DC, c.TB], BF16, "hT"), Buf()) for _ in range(1)])
            ptr = Rot([(pst(st, [128, 8, 128], BF16, "pT"), Buf()) for _ in range(2)])
            if resid is not None:
                yr = Rot([(sbt(st, [128, D], F32, "y"), Buf()) for _ in range(2)])
                gpbc = sbt(st, [128, 2, D], F32, "gpbc")
                gpb = Buf()
                pbc = pst(st, [128, 512], F32)
                pbcb = Buf()
                dg = Rot([(sbt(st, [128, 128], F32, "dg"), Buf()) for _ in range(2)])
                for cd in range(1 if lat_only else 2):
                    for c0 in range(0, DC, 4):
                        for j in range(min(4, DC - c0)):
                            d_, db_ = dg.next()
                            tk.op("dve", lambda d_=d_, j=j: nc.vector.tensor_scalar(
                                out=d_[:], in0=idf[:], scalar1=resid[:, cd, c0 + j:c0 + j + 1], scalar2=None,
                                op0=ALU.mult), reads=[Bc, Bmod], writes=[db_])
                            tk.op("pe", lambda d_=d_, j=j: nc.tensor.matmul(
                                pbc[:, j * 128:(j + 1) * 128], lhsT=ones_f[:], rhs=d_[:], start=True, stop=True),
                                reads=[db_, Bc], writes=[pbcb])
                        n = min(4, DC - c0) * 128
                        tk.op("act", lambda n=n, c0=c0: nc.scalar.copy(out=gpbc[:, cd, c0 * 128:c0 * 128 + n], in_=pbc[:, 0:n]),
                              reads=[pbcb], writes=[gpb])
            tiles = [(t0, tw, tt) for (t0, tw) in c.tblocks(lat_only) for tt in range(tw // 128)]
            state = {}
            cur_h = [None]

            def stage_a(i):
                t0, tw, tt = tiles[i]
                cd = 0 if t0 < S else 1
                r0 = t0 + tt * 128
                x_, xb = xr.next()
                ss, ssb = ssr.next()
                tk.dma("sp", [(x_[:], R[r0:r0 + 128, :])], writes=[xb])
                if resid is not None:
                    y_, yb = yr.next()
                    tk.dma("sp", [(y_[:], Y[r0:r0 + 128, :])], writes=[yb])
                    tk.op("act", lambda: nc.scalar.activation(out=sq[:], in_=y_[:], func=AF.Square, accum_out=ss[:, 0:1]),
                          reads=[yb], writes=[sqb, ssb])
                    rsqrt_ops(ss[:, 1:2], ss[:, 0:1], 1.0 / D, [ssb], [ssb])
                    tk.op("dve", lambda: nc.vector.scalar_tensor_tensor(
                        out=y_[:], in0=y_[:], scalar=ss[:, 1:2], in1=gpbc[:, cd, :], op0=ALU.mult, op1=ALU.mult),
                        reads=[ssb, gpb, yb], writes=[yb])
                    tk.op("pool", lambda: nc.gpsimd.tensor_tensor(out=x_[:], in0=x_[:], in1=y_[:], op=ALU.add),
                          reads=[xb, yb], writes=[xb])
                    tk.dma("pool", [(R[r0:r0 + 128, :], x_[:])], reads=[xb])
                state[i] = (x_, xb, ss, ssb, cd)

            def stage_b(i):
                t0, tw, tt = tiles[i]
                x_, xb, ss, ssb, cd = state.pop(i)
                if gs is None:
                    return
                if tt == 0:
                    cur_h[0] = hr.next()
                hT, hb = cur_h[0]
                tk.op("act", lambda: nc.scalar.activation(out=sq[:], in_=x_[:], func=AF.Square, accum_out=ss[:, 2:3]),
                      reads=[xb], writes=[sqb, ssb])
                rsqrt_ops(ss[:, 3:4], ss[:, 2:3], 1.0 / D, [ssb], [ssb])
                xn, xnb = xnr.next()
                tk.op("dve", lambda: nc.vector.tensor_scalar(out=xn[:], in0=x_[:], scalar1=ss[:, 3:4], scalar2=None,
                                                             op0=ALU.mult), reads=[xb, ssb], writes=[xnb])
                for c0 in range(0, DC, 8):
                    n = min(8, DC - c0)
                    pT, pTb = ptr.next()
                    tk.group("pe", [(lambda j=j: nc.tensor.transpose(pT[:, j, :], xn[:, (c0 + j) * 128:(c0 + j + 1) * 128], idb[:]))
                                    for j in range(n)], reads=[xnb, Bc], writes=[pTb])
                    for j in range(n):
                        cc = c0 + j
                        if j % 2 == 0:
                            tk.op("act", lambda j=j, cc=cc: nc.scalar.activation(
                                out=hT[:, cc, tt * 128:(tt + 1) * 128], in_=pT[:, j, :], func=AF.Identity,
                                scale=gs[:, cd, cc:cc + 1], bias=modT[:, sh_idx, cd, cc:cc + 1]),
                                reads=[pTb, Bmod], writes=[hb])
                        else:
                            tk.op("dve", lambda j=j, cc=cc: nc.vector.tensor_scalar(
                                out=hT[:, cc, tt * 128:(tt + 1) * 128], in0=pT[:, j, :],
                                scalar1=gs[:, cd, cc:cc + 1], scalar2=modT[:, sh_idx, cd, cc:cc + 1],
                                op0=ALU.mult, op1=ALU.add), reads=[pTb, Bmod], writes=[hb])
                if tt == tw // 128 - 1:
                    tk.dma("act", [(fm(HT)[:, ko:ko + kn, t0:t0 + tw], hT[:, ko:ko + kn, 0:tw]) for ko, kn in split_k(DC)],
                           reads=[hb])
            stage_a(0)
            for i in range(len(tiles)):
                if i + 1 < len(tiles):
                    stage_a(i + 1)
                stage_b(i)
        tk.barrier()

    def store_fm(stg, dst_rows_fn):
        def epi(pt, pb, m, tw, info):
            s_, sb_ = stg.next()
            evac_copy(alt(), s_[0:m, 0:tw], pt[0:m, 0:tw], [pb], [sb_])
            dst = dst_rows_fn(info)
            tk.dma("act", [(dst[:, info["t0"]:info["t0"] + tw], s_[0:m, 0:tw])], reads=[sb_])
        return epi

    def phase_win(l):
        with ExitStack() as st:
            banks = fp_banks(st, 6)
            stg = Stager(st, 3, [128, c.TB], BF16)
            stf = Stager(st, 2, [128, c.TB], F32)
            sgr = Stager(st, 2, [128, c.TB], F32)
            nbl = []
            nbl += simple_nblocks(w_in[l], 0, c.FW, lambda col: ("u", col))
            for j0 in range(0, c.CW // 128, 2):
                nj = min(2, c.CW // 128 - j0)
                segs = [(c.OFF_C + j0 * 128, nj * 128), (c.OFF_C + c.CW + j0 * 128, nj * 128)]
                subs = []
                for j in range(nj):
                    subs.append((j * 128, 128, ("a", j0 + j)))
                    subs.append((nj * 128 + j * 128, 128, ("g", j0 + j)))
                nbl.append(dict(width=2 * nj * 128, load=w_segs(w_in[l], segs), subs=subs))
            nbl += simple_nblocks(w_in[l], c.OFF_Q, c.DIN - c.OFF_Q, lambda col: ("u", col))
            nbl += simple_nblocks(w_gate_a[l], 0, c.GR, lambda col: ("u", c.DINP + col))
            held = {}

            def epi(pt, pb, m, tw, info):
                kind, v = info["tag"]
                t0 = info["t0"]
                if kind == "u":
                    s_, sb_ = stg.next()
                    evac_copy(alt(), s_[0:m, 0:tw], pt[0:m, 0:tw], [pb], [sb_])
                    tk.dma("act", [(UT[v:v + m, t0:t0 + tw], s_[0:m, 0:tw])], reads=[sb_])
                elif kind == "a":
                    held["a"] = (pt, pb)
                else:
                    apt, apb = held.pop("a")
                    sg, sgb = sgr.next()
                    tk.op("act", lambda: nc.scalar.activation(out=sg[:, 0:tw], in_=pt[:, 0:tw], func=AF.Sigmoid),
                          reads=[pb], writes=[sgb])
                    s_, sb_ = stf.next()
                    tk.op("dve", lambda: nc.vector.tensor_tensor(out=s_[:, 0:tw], in0=apt[:, 0:tw], in1=sg[:, 0:tw], op=ALU.mult),
                          reads=[apb, sgb], writes=[sb_])
                    tk.dma("act", [(VT[v * 128:(v + 1) * 128, t0:t0 + tw], s_[:, 0:tw])], reads=[sb_])
            gemm("FM", [(0, DC)], a_from(HT), c.tblocks(), nbl, epi, st, banks)
        tk.barrier()

    def phase_stats(l):
        with ExitStack() as st:
            ps1 = pst(st, [128, 512], F32)
            ps1b = Buf()
            ps2 = pst(st, [128, 512], F32)
            ps2b = Buf()
            pr = pst(st, [128, 512], F32)
            prb = Buf()
            rope = sbt(st, [64, 2, T], F32)
            rpb = Buf()
            tk.dma("sp", [(rope[:, 0, :], rope_in[0, 0:64, :]), (rope[:, 1, :], rope_in[1, 0:64, :])], writes=[rpb])
            for (row0, K, gname, CG, rst, rb) in ((c.OFF_Q, c.QL, "q_norm_g", CQG, rstd_q, Brq),
                                                  (c.OFF_KV, c.KVL, "kv_norm_g", CKVG, rstd_kv, Brkv)):
                kc = K // 128
                st2 = st.enter_context(ExitStack()) if False else ExitStack()
                xr = Rot([(sbt(st2, [128, kc, c.TB], BF16), Buf()) for _ in range(2)])
                sqr = Rot([(sbt(st2, [128, kc, c.TB], BF16), Buf()) for _ in range(2)])
                gr = Rot([(sbt(st2, [128, kc, c.TB], BF16), Buf()) for _ in range(2)])
                for (t0, tw) in c.tblocks():
                    x_, xb = xr.next()
                    s_, sb_ = sqr.next()
                    g_, gb = gr.next()
                    tk.dma("sp", [(x_[:, :, 0:tw], fm(UT)[:, row0 // 128:row0 // 128 + kc, t0:t0 + tw])], writes=[xb])
                    tk.op("act", lambda: nc.scalar.activation(out=s_[:, :, 0:tw], in_=x_[:, :, 0:tw], func=AF.Square),
                          reads=[xb], writes=[sb_])
                    tk.group("pe", [mm(ps1[:, 0:tw], ones_b[:], s_[:, k, 0:tw], k == 0, k == kc - 1) for k in range(kc)],
                             reads=[sb_, Bc], writes=[ps1b])
                    rsqrt_ops(rst[:, t0:t0 + tw], ps1[:, 0:tw], 1.0 / K, [ps1b], [rb])
                    if K == c.KVL:
                        for tt in range(tw // 128):
                            tk.group("pe", [mm(ps2[:, 0:1], s_[:, k, tt * 128:(tt + 1) * 128], ones_b[:, 0:1], k == 0, k == kc - 1)
                                            for k in range(kc)], reads=[sb_, Bc], writes=[ps2b])
                            ti = t0 // 128 + tt
                            rsqrt_ops(rstd_kv_tm[:, ti:ti + 1], ps2[:, 0:1], 1.0 / K, [ps2b], [rb])
                    for k in range(kc):
                        tk.op("dve", lambda k=k: nc.vector.tensor_scalar(
                            out=g_[:, k, 0:tw], in0=x_[:, k, 0:tw], scalar1=vcol(gname, k), scalar2=None, op0=ALU.mult),
                            reads=[xb, Bvec], writes=[gb])
                    tk.dma("act", [(fm(CG)[:, :, t0:t0 + tw], g_[:, :, 0:tw])], reads=[gb])
                tk.barrier()
                st2.close()
            kr0 = c.OFF_KV + c.KVL
            xr = Rot([(sbt(st, [64, c.TB], BF16), Buf()) for _ in range(2)])
            t1r = Rot([(sbt(st, [64, c.TB], F32), Buf()) for _ in range(2)])
            orr = Rot([(sbt(st, [64, c.TB], BF16), Buf()) for _ in range(2)])
            for (t0, tw) in c.tblocks():
                x_, xb = xr.next()
                t1, t1b = t1r.next()
                o_, ob = orr.next()
                tk.dma("sp", [(x_[:, 0:tw], UT[kr0:kr0 + 64, t0:t0 + tw])], writes=[xb])
                tk.op("pe", lambda: nc.tensor.matmul(pr[0:64, 0:tw], lhsT=permb[0:64, 0:64], rhs=x_[:, 0:tw], start=True, stop=True),
                      reads=[xb, Bc], writes=[prb])
                tk.op("dve", lambda: nc.vector.tensor_tensor(out=t1[:, 0:tw], in0=x_[:, 0:tw], in1=rope[:, 0, t0:t0 + tw], op=ALU.mult),
                      reads=[xb, rpb], writes=[t1b])
                tk.op("dve", lambda: nc.vector.tensor_tensor(out=o_[:, 0:tw], in0=pr[0:64, 0:tw], in1=rope[:, 1, t0:t0 + tw], op=ALU.mult),
                      reads=[prb, rpb], writes=[ob])
                tk.op("dve", lambda: nc.vector.tensor_tensor(out=o_[:, 0:tw], in0=o_[:, 0:tw], in1=t1[:, 0:tw], op=ALU.add),
                      reads=[ob, t1b], writes=[ob])
                tk.dma("act", [(KRT[:, t0:t0 + tw], o_[:, 0:tw])], reads=[ob])
        tk.barrier()

    def phase_q(l, lat_only):
        with ExitStack() as st:
            banks = fp_banks(st, 5)
            pr = pst(st, [128, 512], F32)
            prb = Buf()
            stg = Stager(st, 3, [128, c.TB], BF16)
            xrr = Rot([(sbt(st, [128, c.TB], BF16), Buf()) for _ in range(2)])
            t1r = Rot([(sbt(st, [128, c.TB], F32), Buf()) for _ in range(2)])
            rope = sbt(st, [128, 2, T], F32)
            rpb = Buf()
            tk.dma("sp", [(rope[:, 0, :], rope_in[0]), (rope[:, 1, :], rope_in[1])], writes=[rpb])
            nbl = []
            for h0 in range(0, c.H, 4):
                nh = min(4, c.H - h0)
                nbl.append(dict(width=nh * 128, load=w_segs(w_uq[l], [((h0 + i) * 192, 128) for i in range(nh)]),
                                subs=[(i * 128, 128, ("n", h0 + i)) for i in range(nh)]))
            for h0 in range(0, c.H, 8):
                nh = min(8, c.H - h0)
                nbl.append(dict(width=nh * 64, load=w_segs(w_uq[l], [((h0 + i) * 192 + 128, 64) for i in range(nh)]),
                                subs=[(j * 128, 128, ("r", h0 + 2 * j)) for j in range(nh // 2)]))
            QR0 = c.H * 128

            def epi(pt, pb, m, tw, info):
                kind, h = info["tag"]
                t0 = info["t0"]
                if kind == "n":
                    s_, sb_ = stg.next()
                    tk.op("dve", lambda: nc.vector.tensor_tensor(out=s_[:, 0:tw], in0=pt[:, 0:tw], in1=rstd_q[:, t0:t0 + tw], op=ALU.mult),
                          reads=[pb, Brq], writes=[sb_])
                    tk.dma("act", [(QT[h * 128:(h + 1) * 128, t0:t0 + tw], s_[:, 0:tw])], reads=[sb_])
                else:
                    x_, xb = xrr.next()
                    t1, t1b = t1r.next()
                    s_, sb_ = stg.next()
                    tk.op("dve", lambda: nc.vector.tensor_tensor(out=x_[:, 0:tw], in0=pt[:, 0:tw], in1=rstd_q[:, t0:t0 + tw], op=ALU.mult),
                          reads=[pb, Brq], writes=[xb])
                    tk.op("pe", lambda: nc.tensor.matmul(pr[:, 0:tw], lhsT=permb[:], rhs=x_[:, 0:tw], start=True, stop=True),
                          reads=[xb, Bc], writes=[prb])
                    tk.op("dve", lambda: nc.vector.tensor_tensor(out=t1[:, 0:tw], in0=x_[:, 0:tw], in1=rope[:, 0, t0:t0 + tw], op=ALU.mult),
                          reads=[xb, rpb], writes=[t1b])
                    tk.op("dve", lambda: nc.vector.tensor_tensor(out=s_[:, 0:tw], in0=pr[:, 0:tw], in1=rope[:, 1, t0:t0 + tw], op=ALU.mult),
                          reads=[prb, rpb], writes=[sb_])
                    tk.op("dve", lambda: nc.vector.tensor_tensor(out=s_[:, 0:tw], in0=s_[:, 0:tw], in1=t1[:, 0:tw], op=ALU.add),
                          reads=[sb_, t1b], writes=[sb_])
                    tk.dma("act", [(QT[QR0 + h * 64:QR0 + h * 64 + 128, t0:t0 + tw], s_[:, 0:tw])], reads=[sb_])
            gemm("FM", [(0, c.QL // 128)], a_from(CQG), c.tblocks(lat_only), nbl, epi, st, banks)
        tk.barrier()

    def phase_kv(l):
        with ExitStack() as st:
            banks = fp_banks(st, 6)
            stg = Stager(st, 3, [128, 512], BF16)
            nbl = []
            for h0 in range(0, c.H, 4):
                nh = min(4, c.H - h0)
                segs = [((h0 + i) * 256, 128) for i in range(nh)]
                nbl.append(dict(width=nh * 128, load=w_segs(w_ukv[l], segs),
                                subs=[(i * 128, 128, h0 + i) for i in range(nh)]))

            def epi_k(pt, pb, m, tw, info):
                h = info["tag"]
                t0 = info["t0"]
                s_, sb_ = stg.next()
                tk.op("dve", lambda: nc.vector.tensor_tensor(out=s_[:, 0:tw], in0=pt[:, 0:tw], in1=rstd_kv[:, t0:t0 + tw], op=ALU.mult),
                      reads=[pb, Brkv], writes=[sb_])
                tk.dma("act", [(KNT[h * 128:(h + 1) * 128, t0:t0 + tw], s_[:, 0:tw])], reads=[sb_])
            gemm("FM", [(0, c.KVL // 128)], a_from(CKVG), c.tblocks(), nbl, epi_k, st, banks)
            if os.environ.get("KVHALF"):
                tk.barrier()
                return
            nbl = []
            for h0 in range(0, c.H, 4):
                nh = min(4, c.H - h0)
                segs = [((h0 + i) * 256 + 128, 128) for i in range(nh)]
                nbl.append(dict(width=nh * 128, load=w_segs(w_ukv[l], segs), subs=[(0, nh * 128, h0)]))

            def epi_v(pt, pb, m, n, info):
                h0 = info["tag"]
                ti = info["tt"]
                s_, sb_ = stg.next()
                if os.environ.get("VCOPY"):
                    evac_copy("dve", s_[:, 0:n], pt[:, 0:n], [pb], [sb_])
                else:
                    tk.op("act", lambda: nc.scalar.activation(out=s_[:, 0:n], in_=pt[:, 0:n], func=AF.Identity,
                                                              scale=rstd_kv_tm[:, ti:ti + 1]), reads=[pb, Brkv], writes=[sb_])
                tk.dma("act", [(VV[ti * 128:(ti + 1) * 128, h0 * 128:h0 * 128 + n], s_[:, 0:n])], reads=[sb_])
            gemm("TM", [(0, c.KVL // 128)], a_from(CKVG), c.tblocks(), nbl, epi_v, st, banks)
        tk.barrier()

    def phase_attn(l, lat_only, pump=None):
        with ExitStack() as st:
            sbk = Rot([(pst(st, [128, 512], F32, "ps_s"), Buf()) for _ in range(4)])
            oacc = Rot([(pst(st, [128, 512], F32, "ps_o"), Buf()) for _ in range(2)])
            racc = Rot([(pst(st, [128, 512], F32, "ps_r"), Buf()) for _ in range(2)])
            kr = sbt(st, [64, T], BF16)
            krb = Buf()
            tk.dma("sp", [(kr[:], KRT)], writes=[krb])
            knr = Rot([(sbt(st, [128, T], BF16, "kn"), Buf()) for _ in range(2)])
            vr = Rot([(sbt(st, [128, c.NTT, 128], BF16, "v"), Buf()) for _ in range(2)])
            qnr = Rot([(sbt(st, [128, T], BF16, "qn"), Buf()) for _ in range(2)])
            qrr = Rot([(sbt(st, [64, T], BF16, "qr"), Buf()) for _ in range(2)])
            ptr = Rot([(sbt(st, [128, c.TB], BF16, "pT"), Buf()) for _ in range(5)])
            rir = Rot([(sbt(st, [128, c.TB], F32, "ri"), Buf()) for _ in range(2)])
            stg = Stager(st, 2, [128, c.TB], BF16)
            TQ = S if lat_only else T
            LAG = 3
            for h in range(c.H):
                kn, knb = knr.next()
                v_, vb = vr.next()
                qn, qnb = qnr.next()
                qr, qrb = qrr.next()
                tk.dma("sp", [(kn[:], KNT[h * 128:(h + 1) * 128, :])], writes=[knb])
                tk.dma("sp", [(v_[:], VV.rearrange("(tt p) n -> p tt n", p=128)[:, :, h * 128:(h + 1) * 128])], writes=[vb])
                tk.dma("sp", [(qn[:, 0:TQ], QT[h * 128:(h + 1) * 128, 0:TQ])], writes=[qnb])
                tk.dma("sp", [(qr[:, 0:TQ], QT[c.H * 128 + h * 64:c.H * 128 + (h + 1) * 64, 0:TQ])], writes=[qrb])
                for (t0, tw) in c.tblocks(lat_only):
                    kts = list(range(c.NTT)) if t0 < S else list(range(S // 128, c.NTT))
                    po, pob = oacc.next()
                    prs, prsb = racc.next()
                    nk = len(kts)

                    def emit_pv(item):
                        pT, pTb, kt, i = item
                        tk.group("pe", [mm(po[:, 0:tw], v_[:, kt, :], pT[:, 0:tw], i == 0, i == nk - 1),
                                        mm(prs[:, 0:tw], ones_b[:], pT[:, 0:tw], i == 0, i == nk - 1)],
                                 reads=[pTb, vb, Bc], writes=[pob, prsb], own_acc=(i > 0))
                    pending = []
                    for i, kt in enumerate(kts):
                        pss, pssb = sbk.next()
                        tk.group("pe", [mm(pss[:, 0:tw], kn[:, kt * 128:(kt + 1) * 128], qn[:, t0:t0 + tw], True, False),
                                        mm(pss[:, 0:tw], kr[:, kt * 128:(kt + 1) * 128], qr[:, t0:t0 + tw], False, True)],
                                 reads=[knb, qnb, krb, qrb], writes=[pssb])
                        pT, pTb = ptr.next()
                        tk.op("act", lambda: nc.scalar.activation(out=pT[:, 0:tw], in_=pss[:, 0:tw], func=AF.Exp, scale=c.ATTN_SCALE),
                              reads=[pssb], writes=[pTb])
                        pending.append((pT, pTb, kt, i))
                        if len(pending) > LAG:
                            emit_pv(pending.pop(0))
                        if pump is not None:
                            pump()
                    while pending:
                        emit_pv(pending.pop(0))
                    ri, rib = rir.next()
                    tk.op("dve", lambda: nc.vector.reciprocal(out=ri[:, 0:tw], in_=prs[:, 0:tw]), reads=[prsb], writes=[rib])
                    s_, sb_ = stg.next()
                    tk.op("dve", lambda: nc.vector.tensor_tensor(out=s_[:, 0:tw], in0=po[:, 0:tw], in1=ri[:, 0:tw], op=ALU.mult),
                          reads=[pob, rib], writes=[sb_])
                    tk.dma("act", [(OT[h * 128:(h + 1) * 128, t0:t0 + tw], s_[:, 0:tw])], reads=[sb_])
            if pump is not None:
                pump(True)
        tk.barrier()

    def phase_fourier(l, lat_only):
        with ExitStack() as st:
            banks = fp_banks(st, 6)
            stg = Stager(st, 3, [128, 512], BF16)
            nbl = [dict(width=512, load=w_cols(dftc_in, b0, 512), subs=[(0, 512, b0)]) for b0 in range(0, 2 * c.FW, 512)]

            def epi_ab(pt, pb, m, n, info):
                ti = info["tt"]
                s_, sb_ = stg.next()
                evac_copy(alt(), s_[:, 0:n], pt[:, 0:n], [pb], [sb_])
                tk.dma("act", [(AB[ti * 128:(ti + 1) * 128, info["tag"]:info["tag"] + n], s_[:, 0:n])], reads=[sb_])
            gemm("TM", [(0, c.FW // 128)], a_from(UT), c.tblocks(lat_only), nbl, epi_ab, st, banks)
        tk.barrier()
        with ExitStack() as st:
            banks = fp_banks(st, 6)
            stg = Stager(st, 3, [128, 512], BF16)
            for (tok0, L_, DF) in (((0, S, DFTP),) if lat_only else ((0, S, DFTP), (S, CTX, DFTX))):
                lc = L_ // 128
                abv = AB.rearrange("(tt p) n -> p tt n", p=128)
                nbl = []
                for b0 in range(0, c.FW, 512):
                    w = min(512, c.FW - b0)

                    def load(kc0, kcn, b0=b0, w=w, lc=lc, tok0=tok0):
                        out = []
                        for half in range(2):
                            for ko, kn in split_k(lc):
                                out.append((half * lc + ko, kn, 0, w,
                                            abv[:, tok0 // 128 + ko:tok0 // 128 + ko + kn, half * c.FW + b0:half * c.FW + b0 + w]))
                        return out
                    nbl.append(dict(width=w, load=load, subs=[(o, 128, b0 + o) for o in range(0, w, 128)]))

                def epi_f(pt, pb, m, tw, info, tok0=tok0):
                    s_, sb_ = stg.next()
                    evac_copy(alt(), s_[:, 0:tw], pt[:, 0:tw], [pb], [sb_])
                    r = info["tag"]
                    tk.dma("act", [(FT[r:r + 128, tok0 + info["t0"]:tok0 + info["t0"] + tw], s_[:, 0:tw])], reads=[sb_])
                tbl = [(t0, min(c.TB, L_ - t0)) for t0 in range(0, L_, c.TB)]
                gemm("FM", [(0, 2 * lc)], a_from(DF), tbl, nbl, epi_f, st, banks)
        tk.barrier()

    def conv_alloc(st, lat_only):
        CWC = c.CW // 128
        PAD = CONV_K // 2
        TT_ = S if lat_only else T
        yc = sbt(st, [128, CWC, TT_], F32, "yc")
        ycb = [Buf() for _ in range(CWC)]
        vps = [(sbt(st, [128, T + 4 * PAD], F32, "vp"), Buf()) for _ in range(2)]
        return dict(yc=yc, ycb=ycb, vps=vps)

    def conv_mac_gen(l, lat_only, cs):
        CWC = c.CW // 128
        PAD = CONV_K // 2
        regions = [(0, S)] if lat_only else [(0, S), (S, CTX)]
        yc, ycb, vps = cs["yc"], cs["ycb"], cs["vps"]
        for (vp, vpb) in vps:
            tk.op("pool", lambda vp=vp: nc.gpsimd.memset(vp[:], 0.0), writes=[vpb])
        wo, _ = c.V["conv_w"]

        def load(j):
            vp, vpb = vps[j % 2]
            prs = []
            for ri, (r0, rl) in enumerate(regions):
                off = PAD + r0 + ri * 2 * PAD
                prs.append((vp[:, off:off + rl], VT[j * 128:(j + 1) * 128, r0:r0 + rl]))
            tk.dma("sp", prs, writes=[vpb])
        load(0)
        yield
        for j in range(CWC):
            if j + 1 < CWC:
                load(j + 1)
                yield
            vp, vpb = vps[j % 2]
            for ri, (r0, rl) in enumerate(regions):
                base = r0 + ri * 2 * PAD
                acc = yc[:, j, r0:r0 + rl]
                tk.op("dve", lambda: nc.vector.tensor_scalar(
                    out=acc, in0=vp[:, base:base + rl], scalar1=vecT[:, wo + j:wo + j + 1], scalar2=vcol("conv_b", j),
                    op0=ALU.mult, op1=ALU.add), reads=[vpb, Bvec], writes=[ycb[j]])
                yield
                for k in range(1, CONV_K):
                    tk.op("dve", lambda k=k: nc.vector.scalar_tensor_tensor(
                        out=acc, in0=vp[:, base + k:base + k + rl], scalar=vecT[:, wo + k * CWC + j:wo + k * CWC + j + 1],
                        in1=acc, op0=ALU.mult, op1=ALU.add), reads=[vpb, Bvec, ycb[j]], writes=[ycb[j]])
                    yield

    def phase_conv_ln(l, lat_only, cs):
        CWC = c.CW // 128
        yc, ycb = cs["yc"], cs["ycb"]
        with ExitStack() as st:
            pm = pst(st, [128, 512], F32)
            pmb = Buf()
            pq = pst(st, [128, 512], F32)
            pqb = Buf()
            sqr = Rot([(sbt(st, [128, c.TB], F32, "sq"), Buf()) for _ in range(2)])
            mean = sbt(st, [128, c.TB], F32)
            rstd = sbt(st, [128, c.TB], F32)
            msq = sbt(st, [128, c.TB], F32)
            stb = Buf()
            zr = Rot([(sbt(st, [128, c.TB], F32, "z"), Buf()) for _ in range(2)])
            stg = Stager(st, 3, [128, c.TB], BF16)
            for (t0, tw) in c.tblocks(lat_only):
                tk.group("pe", [mm(pm[:, 0:tw], ones_f[:], yc[:, j, t0:t0 + tw], j == 0, j == CWC - 1) for j in range(CWC)],
                         reads=ycb + [Bc], writes=[pmb])
                for j in range(CWC):
                    s_, sb_ = sqr.next()
                    tk.op("act", lambda j=j: nc.scalar.activation(out=s_[:, 0:tw], in_=yc[:, j, t0:t0 + tw], func=AF.Square),
                          reads=[ycb[j]], writes=[sb_])
                    tk.group("pe", [mm(pq[:, 0:tw], ones_f[:], s_[:, 0:tw], j == 0, j == CWC - 1)], reads=[sb_, Bc], writes=[pqb],
                             own_acc=(j > 0))
                tk.op("act", lambda: nc.scalar.mul(out=mean[:, 0:tw], in_=pm[:, 0:tw], mul=1.0 / c.CW), reads=[pmb], writes=[stb])
                tk.op("dve", lambda: nc.vector.tensor_tensor(out=msq[:, 0:tw], in0=mean[:, 0:tw], in1=mean[:, 0:tw], op=ALU.mult),
                      reads=[stb], writes=[stb])
                tk.op("dve", lambda: nc.vector.scalar_tensor_tensor(out=rstd[:, 0:tw], in0=pq[:, 0:tw], scalar=1.0 / c.CW, in1=msq[:, 0:tw],
                                                                    op0=ALU.mult, op1=ALU.subtract), reads=[pqb, stb], writes=[stb])
                rsqrt_ops(rstd[:, 0:tw], rstd[:, 0:tw], 1.0, [stb], [stb])
                for j in range(CWC):
                    z, zb = zr.next()
                    tk.op("dve", lambda j=j: nc.vector.tensor_tensor(out=z[:, 0:tw], in0=yc[:, j, t0:t0 + tw], in1=mean[:, 0:tw], op=ALU.subtract),
                          reads=[ycb[j], stb], writes=[zb])
                    tk.op("dve", lambda: nc.vector.tensor_tensor(out=z[:, 0:tw], in0=z[:, 0:tw], in1=rstd[:, 0:tw], op=ALU.mult),
                          reads=[zb, stb], writes=[zb])
                    s_, sb_ = stg.next()
                    tk.op("act", lambda j=j: nc.scalar.activation(out=s_[:, 0:tw], in_=z[:, 0:tw], func=AF.Silu,
                                                                  scale=vcol("conv_ln_g", j), bias=vcol("conv_ln_b", j)),
                          reads=[zb, Bvec], writes=[sb_])
                    tk.dma("act", [(CT[j * 128:(j + 1) * 128, t0:t0 + tw], s_[:, 0:tw])], reads=[sb_])
        tk.barrier()

    def phase_merge(l, lat_only):
        FWC, CWC, MWC, GRC = c.FW // 128, c.CW // 128, c.MW // 128, c.GR // 128
        KA = FWC + CWC + MWC + GRC
        NBW = 256
        with ExitStack() as st:
            banks = fp_banks(st, 8)
            wr = Rot([(sbt(st, [128, FWC + CWC + MWC + 3 * GRC, NBW], BF16, "Wm"), Buf()) for _ in range(2)])
            ar = Rot([(sbt(st, [128, KA, c.TB], BF16, "Am"), Buf()) for _ in range(2)])
            gr = Rot([(sbt(st, [128, 3, c.TB], F32, "g"), Buf()) for _ in range(2)])
            mr = Rot([(sbt(st, [128, 2, c.TB], F32, "m"), Buf()) for _ in range(2)])
            stg = Stager(st, 3, [128, c.TB], BF16)
            bgo, _ = c.V["b_gate"]
            srcs = [(FT, FWC), (CT, CWC), (OT, MWC)]
            for n0 in range(0, D, NBW):
                W, Wb = wr.next()
                prs = []
                ko = 0
                for wd, kc in ((w_pf[l], FWC), (w_pc[l], CWC), (w_pm[l], MWC)):
                    for k0, kn in split_k(kc):
                        prs.append((W[:, ko + k0:ko + k0 + kn, :], fm(wd)[:, k0:k0 + kn, n0:n0 + NBW]))
                    ko += kc
                for i in range(3):
                    prs.append((W[:, ko:ko + GRC, :], fm(w_gate_b[l])[:, :, i * D + n0:i * D + n0 + NBW]))
                    ko += GRC
                tk.dma("pool", prs, writes=[Wb])
                for (t0, tw) in c.tblocks(lat_only):
                    A, Ab = ar.next()
                    prs = []
                    ko = 0
                    for sd, kc in srcs:
                        for k0, kn in split_k(kc):
                            prs.append((A[:, ko + k0:ko + k0 + kn, 0:tw], fm(sd)[:, k0:k0 + kn, t0:t0 + tw]))
                        ko += kc
                    prs.append((A[:, ko:ko + GRC, 0:tw], fm(UT)[:, c.DINP // 128:c.DINP // 128 + GRC, t0:t0 + tw]))
                    tk.dma("sp", prs, writes=[Ab])
                    for cs in range(NBW // 128):
                        cc = n0 // 128 + cs
                        cols = slice(cs * 128, (cs + 1) * 128)
                        ys = []
                        ko = 0
                        wko = 0
                        for kc in (FWC, CWC, MWC):
                            pt, pb = banks.next()
                            tk.group("pe", [mm(pt[:, 0:tw], W[:, wko + k, cols], A[:, ko + k, 0:tw], k == 0, k == kc - 1)
                                            for k in range(kc)], reads=[Wb, Ab], writes=[pb])
                            ys.append((pt, pb))
                            ko += kc
                            wko += kc
                        g_, gb = gr.next()
                        for i in range(3):
                            pt, pb = banks.next()
                            tk.group("pe", [mm(pt[:, 0:tw], W[:, wko + k, cols], A[:, ko + k, 0:tw], k == 0, k == GRC - 1)
                                            for k in range(GRC)], reads=[Wb, Ab], writes=[pb])
                            wko += GRC
                            tk.op("act", lambda i=i, pt=pt: nc.scalar.activation(
                                out=g_[:, i, 0:tw], in_=pt[:, 0:tw], func=AF.Sigmoid,
                                bias=vecT[:, bgo + i * DC + cc:bgo + i * DC + cc + 1], scale=1.0),
                                reads=[pb, Bvec], writes=[gb])
                        m_, mb = mr.next()
                        tk.op("dve", lambda: nc.vector.tensor_tensor(out=m_[:, 0, 0:tw], in0=ys[0][0][:, 0:tw], in1=g_[:, 0, 0:tw], op=ALU.mult),
                              reads=[ys[0][1], gb], writes=[mb])
                        tk.op("dve", lambda: nc.vector.tensor_tensor(out=m_[:, 1, 0:tw], in0=ys[1][0][:, 0:tw], in1=g_[:, 1, 0:tw], op=ALU.mult),
                              reads=[ys[1][1], gb], writes=[mb])
                        tk.op("dve", lambda: nc.vector.tensor_tensor(out=m_[:, 0, 0:tw], in0=m_[:, 0, 0:tw], in1=m_[:, 1, 0:tw], op=ALU.add),
                              reads=[mb], writes=[mb])
                        tk.op("dve", lambda: nc.vector.tensor_tensor(out=m_[:, 1, 0:tw], in0=ys[2][0][:, 0:tw], in1=g_[:, 2, 0:tw], op=ALU.mult),
                              reads=[ys[2][1], gb, mb], writes=[mb])
                        s_, sb_ = stg.next()
                        tk.op("dve", lambda: nc.vector.tensor_tensor(out=s_[:, 0:tw], in0=m_[:, 0, 0:tw], in1=m_[:, 1, 0:tw], op=ALU.add),
                              reads=[mb], writes=[sb_])
                        tk.dma("act", [(MT[cc * 128:(cc + 1) * 128, t0:t0 + tw], s_[:, 0:tw])], reads=[sb_])
        tk.barrier()

    def phase_gemm_to_Y(l, Wd, ATd, K, lat_only):
        kc = K // 128
        kgs = [(k0, min(c.KG, kc - k0)) for k0 in range(0, kc, c.KG)]
        with ExitStack() as st:
            banks = fp_banks(st, 6)
            nbl = [dict(width=512, load=w_cols(Wd, b0, 512), subs=[(0, 512, b0)]) for b0 in range(0, D, 512)]
            if len(kgs) == 1:
                stg = Stager(st, 4, [128, 512], F32)

                def epi(pt, pb, m, n, info):
                    ti = info["tt"]
                    s_, sb_ = stg.next()
                    evac_copy(alt(), s_[:, 0:n], pt[:, 0:n], [pb], [sb_])
                    tk.dma("act", [(Y[ti * 128:(ti + 1) * 128, info["tag"]:info["tag"] + n], s_[:, 0:n])], reads=[sb_])
            else:
                ntt = (S if lat_only else T) // 128
                acc = sbt(st, [128, ntt, 512], F32, "acc")
                accb = [Buf() for _ in range(ntt)]

                def epi(pt, pb, m, n, info):
                    ti = info["tt"]
                    if info["kg"] == 0:
                        tk.op("act", lambda: nc.scalar.copy(out=acc[:, ti, 0:n], in_=pt[:, 0:n]), reads=[pb], writes=[accb[ti]])
                    else:
                        tk.op("dve", lambda: nc.vector.tensor_tensor(out=acc[:, ti, 0:n], in0=acc[:, ti, 0:n], in1=pt[:, 0:n], op=ALU.add),
                              reads=[pb, accb[ti]], writes=[accb[ti]])
                    if info["kg"] == info["nkg"] - 1:
                        tk.dma("act", [(Y[ti * 128:(ti + 1) * 128, info["tag"]:info["tag"] + n], acc[:, ti, 0:n])], reads=[accb[ti]])
            gemm("TM", kgs, a_from(ATd), c.tblocks(lat_only), nbl, epi, st, banks)
        tk.barrier()

    def phase_ff1(l, lat_only):
        with ExitStack() as st:
            banks = fp_banks(st, 6)
            rr = Rot([(sbt(st, [128, c.TB], F32, "r"), Buf()) for _ in range(3)])
            stg = Stager(st, 3, [128, c.TB], BF16)
            nbl = simple_nblocks(w_ff1[l], 0, c.DFF, lambda col: col)

            def epi(pt, pb, m, tw, info):
                r_, rb = rr.next()
                tk.op("act", lambda: nc.scalar.activation(out=r_[:, 0:tw], in_=pt[:, 0:tw], func=AF.Relu), reads=[pb], writes=[rb])
                s_, sb_ = stg.next()
                tk.op("dve", lambda: nc.vector.tensor_tensor(out=s_[:, 0:tw], in0=r_[:, 0:tw], in1=r_[:, 0:tw], op=ALU.mult),
                      reads=[rb], writes=[sb_])
                r0 = info["tag"]
                tk.dma("act", [(HID[r0:r0 + 128, info["t0"]:info["t0"] + tw], s_[:, 0:tw])], reads=[sb_])
            gemm("FM", [(0, DC)], a_from(HT), c.tblocks(lat_only), nbl, epi, st, banks)
        tk.barrier()

    gpm_prev = sbt(P, [128, 2, DC], F32, "gpm_prev")

    def run_layer(l):
        last = l == L - 1
        if l == 0:
            phase_vectors(l)
            if stop == "vectors": return True
            phase_modulate(l, gsa, 0)
        if stop == "mod1": return True
        phase_win(l)
        if stop == "win": return True
        with ExitStack() as lst:
            cs = conv_alloc(lst, last)
            gen = conv_mac_gen(l, last, cs)
            n_steps = c.H * sum((c.NTT if t0 < S else CTX // 128) for (t0, tw) in c.tblocks(last))
            n_ops = (c.CW // 128) * (2 + (1 if last else 2) * CONV_K) + 2
            rate = 1.15 * n_ops / n_steps
            credit = [0.0]

            def pump(drain=False):
                if drain:
                    for _ in gen:
                        pass
                    return
                credit[0] += rate
                while credit[0] >= 1.0:
                    credit[0] -= 1.0
                    try:
                        next(gen)
                    except StopIteration:
                        return
            phase_stats(l)
            if stop == "stats": return True
            phase_q(l, last)
            if stop == "q": return True
            phase_kv(l)
            if stop == "qkv": return True
            phase_attn(l, last, pump)
            if stop == "attn": return True
            phase_conv_ln(l, last, cs)
        if stop == "conv": return True
        phase_fourier(l, last)
        if stop == "fourier": return True
        phase_merge(l, last)
        if stop == "merge": return True
        phase_gemm_to_Y(l, w_out[l], MT, D, last)
        if stop == "wout": return True
        phase_modulate(l, gsm, 3, lat_only=last, resid=gpa)
        if stop == "mod2": return True
        phase_ff1(l, last)
        if stop == "ff1": return True
        phase_gemm_to_Y(l, w_ff2[l], HID, c.DFF, last)
        if stop == "ff2": return True
        if last:
            phase_modulate(l, None, 0, lat_only=True, resid=gpm)
        else:
            tk.op("dve", lambda: nc.vector.tensor_copy(out=gpm_prev[:], in_=gpm[:]), reads=[Bmod], writes=[Bmod])
            tk.barrier()
            phase_vectors(l + 1)
            phase_modulate(l + 1, gsa, 0, lat_only=False, resid=gpm_prev)
        return False

    for l in range(L):
        if run_layer(l):
            break
    rows = 256
    tk.dma("sp", [(y_out[r0:r0 + rows, :], R[r0:r0 + rows, :]) for r0 in range(0, S, rows)])
    tk.barrier()
    return nc, tk


def make_consts(cfg):
    c = cfg
    S, CTX, T = c.S, c.CTX, c.T
    out = {}
    out["ident"] = np.eye(128, dtype=np.float32)
    perm = np.zeros((128, 128), np.float32)
    for m in range(128):
        blk, mm_ = divmod(m, 64)
        a, r = divmod(mm_, 32)
        hf, f = divmod(r, 16)
        perm[blk * 64 + a * 32 + (1 - hf) * 16 + f, m] = 1.0
    out["perm"] = perm
    nf = 16
    inv_freq = np.power(np.float32(ROPE_THETA), -np.arange(nf, dtype=np.float32) / np.float32(nf)).astype(np.float32)
    pos = np.stack([np.repeat(np.arange(S // GRID_W, dtype=np.float32), GRID_W),
                    np.tile(np.arange(GRID_W, dtype=np.float32), S // GRID_W)], -1)
    ang = (pos[:, :, None] * inv_freq).astype(np.float32)
    cosv, sinv = np.cos(ang), np.sin(ang)
    rope = np.zeros((2, 128, T), np.float32)
    rope[0, :, S:] = 1.0
    for blk in range(2):
        for a in range(2):
            for hf in range(2):
                rows = slice(blk * 64 + a * 32 + hf * 16, blk * 64 + a * 32 + hf * 16 + 16)
                rope[0, rows, :S] = cosv[:, a, :].T
                rope[1, rows, :S] = (sinv[:, a, :].T) * (-1.0 if hf == 0 else 1.0)
    out["rope"] = rope

    def dft(n):
        k = np.arange(n, dtype=np.float64)
        a = 2.0 * np.pi * np.outer(k, k) / n
        return np.cos(a) / np.sqrt(n), np.sin(a) / np.sqrt(n)
    cc, sc = dft(c.FGW)
    dftc = np.zeros((c.FW, 2 * c.FW), np.float32)
    for g in range(4):
        sl = slice(g * c.FGW, (g + 1) * c.FGW)
        dftc[sl, sl] = cc
        dftc[sl, c.FW + g * c.FGW:c.FW + (g + 1) * c.FGW] = sc
    out["dftc"] = dftc
    cl, sl_ = dft(S)
    out["dftp"] = np.concatenate([cl, -sl_], 0).astype(np.float32)
    cx, sx = dft(CTX)
    out["dftx"] = np.concatenate([cx, -sx], 0).astype(np.float32)
    return out


def pack_vecs(cfg, p):
    c = cfg
    rows = []
    for l in range(c.DEPTH):
        parts = [p[n][l].reshape(-1, 128) for n in ("g_mix_pre", "g_mix_post", "g_mlp_pre", "g_mlp_post", "b_mod",
                                                     "conv_b", "conv_ln_g", "conv_ln_b", "q_norm_g", "kv_norm_g", "b_gate")]
        parts.append(p["conv_w"][l].reshape(-1, 128))
        rows.append(np.concatenate(parts, 0))
    v = np.ascontiguousarray(np.stack(rows, 0), dtype=np.float32)
    assert v.shape == (c.DEPTH, c.NV, 128), v.shape
    return v


WEIGHTS = ("w_mod_a", "w_mod_b", "w_in", "w_uq", "w_ukv", "w_pf", "w_pc", "w_pm", "w_gate_a", "w_gate_b",
           "w_out", "w_ff1", "w_ff2")


def make_in_maps(cfg, inputs):
    c = cfg
    B = inputs["x"].shape[0]
    consts = make_consts(c)
    vecs = pack_vecs(c, inputs)
    shared = {k: np.ascontiguousarray(inputs[k], dtype=np.float32) for k in WEIGHTS}
    shared.update(consts)
    shared["vecs"] = vecs
    maps = []
    for b in range(B):
        m = dict(shared)
        m["xin"] = np.ascontiguousarray(np.concatenate([inputs["x"][b], inputs["ctx"][b]], 0), dtype=np.float32)
        m["cond"] = np.ascontiguousarray(np.stack([inputs["c"][b], inputs["c_ctx"]], 0).reshape(2 * c.DC, 128), dtype=np.float32)
        maps.append(m)
    return maps


def kernel(**inputs):
    cfg = Cfg()
    nc, tk = build_program(cfg)
    maps = make_in_maps(cfg, inputs)
    res = run_bass_kernel_spmd(nc, maps, core_ids=list(range(len(maps))))
    return np.stack([r["y"] for r in res.results], 0).astype(np.float32)
```

```python
import numpy as np
from contextlib import ExitStack
import concourse.bass as bass
import concourse.mybir as mybir
from concourse.bass_utils import run_bass_kernel_spmd

F32 = mybir.dt.float32
BF16 = mybir.dt.bfloat16
ALU = mybir.AluOpType
AF = mybir.ActivationFunctionType

SEM_LIMIT = 30000
import os
DBG_NOROPE = bool(os.environ.get('NOROPE'))
EPS = 1e-6
CONV_K = 31
GRID_W = 64
ROPE_THETA = 10000.0


class Cfg:
    def __init__(s, D=4096, S=2048, CTX=256, DEPTH=4):
        s.D, s.S, s.CTX, s.DEPTH = D, S, CTX, DEPTH
        s.T = S + CTX
        s.DC = D // 128
        s.FGW = D // 16
        s.FW = 4 * s.FGW
        s.CW = D // 4
        s.H = D // 256
        s.QL = D // 4
        s.KVL = D // 8
        s.MW = s.H * 128
        s.GR = 512
        s.MR = 256
        s.DFF = 4 * D
        s.OFF_C = s.FW
        s.OFF_Q = s.OFF_C + 2 * s.CW
        s.OFF_KV = s.OFF_Q + s.QL
        s.DIN = s.OFF_KV + s.KVL + 64
        s.DINP = ((s.DIN + 127) // 128) * 128
        s.KG = 32
        s.TB = 512
        s.NTT = s.T // 128
        s.ATTN_SCALE = float((128 + 64) ** -0.5)
        o = 0
        s.V = {}
        for nm, n in (("g_mix_pre", s.DC), ("g_mix_post", s.DC), ("g_mlp_pre", s.DC), ("g_mlp_post", s.DC),
                      ("b_mod", 6 * s.DC), ("conv_b", s.CW // 128), ("conv_ln_g", s.CW // 128),
                      ("conv_ln_b", s.CW // 128), ("q_norm_g", s.QL // 128), ("kv_norm_g", s.KVL // 128),
                      ("b_gate", 3 * s.DC), ("conv_w", CONV_K * (s.CW // 128))):
            s.V[nm] = (o, n)
            o += n
        s.NV = o

    def tblocks(s, lat_only=False):
        bl = [(t0, min(s.TB, s.S - t0)) for t0 in range(0, s.S, s.TB)]
        if not lat_only:
            bl += [(s.S + t0, min(s.TB, s.CTX - t0)) for t0 in range(0, s.CTX, s.TB)]
        return bl


class Buf:
    __slots__ = ("name", "w_evs", "r_evs")

    def __init__(self, name=""):
        self.name = name
        self.w_evs = {}
        self.r_evs = {}


class TK:
    def __init__(self, nc):
        self.nc = nc
        self.stack = ExitStack()
        self.eng = {"pe": nc.tensor, "act": nc.scalar, "dve": nc.vector, "pool": nc.gpsimd, "sp": nc.sync}
        self.cur = {}
        self.waited = {e: {} for e in self.eng}
        self.nsem = 0
        self.dq = {}
        self.dq_i = {}
        self.last_ev = {}
        self.n_ins = 0
        for q, m in (("sp", 12), ("pool", 8), ("act", 8)):
            self.dq[q] = [[self.new_sem(), 0] for _ in range(m)]
            self.dq_i[q] = 0

    def new_sem(self):
        self.nsem += 1
        return self.stack.enter_context(self.nc.semaphore("s%d" % self.nsem))

    def mark(self, ins, e):
        c = self.cur.get(e)
        if c is None or c[1] >= SEM_LIMIT:
            c = [self.new_sem(), 0]
            self.cur[e] = c
        c[1] += 1
        ins.then_inc(c[0], 1)
        ev = (c[0], c[1], e)
        self.last_ev[id(c[0])] = ev
        return ev

    def wait(self, e, ev):
        sem, val = ev[0], ev[1]
        w = self.waited[e].get(id(sem))
        if w is not None and w >= val:
            return
        self.waited[e][id(sem)] = val
        self.eng[e].wait_ge(sem, val)
        self.n_ins += 1

    def _pre(self, e, reads, writes, own_acc=False):
        for b in reads:
            for ev in b.w_evs.values():
                self.wait(e, ev)
        for b in writes:
            for ev in b.w_evs.values():
                if own_acc and ev[2] == e:
                    continue
                self.wait(e, ev)
            for ev in b.r_evs.values():
                self.wait(e, ev)

    def _post(self, evs, reads, writes):
        for b in reads:
            for ev in evs:
                b.r_evs[id(ev[0])] = ev
        for b in writes:
            b.w_evs = {id(ev[0]): ev for ev in evs}
            b.r_evs = {}

    def op(self, e, fn, reads=(), writes=()):
        self._pre(e, reads, writes)
        ins = fn()
        self.n_ins += 1
        ev = self.mark(ins, e)
        self._post([ev], reads, writes)
        return ev

    def group(self, e, fns, reads=(), writes=(), own_acc=False):
        self._pre(e, reads, writes, own_acc)
        ins = None
        for fn in fns:
            ins = fn()
            self.n_ins += 1
        ev = self.mark(ins, e)
        self._post([ev], reads, writes)
        return ev

    def dma(self, q, pairs, reads=(), writes=()):
        self._pre(q, reads, writes)
        evs = []
        for out, in_ in pairs:
            lst = self.dq[q]
            i = self.dq_i[q]
            self.dq_i[q] = (i + 1) % len(lst)
            slot = lst[i]
            if slot[1] >= SEM_LIMIT:
                self.wait(q, (slot[0], slot[1]))
                slot[0] = self.new_sem()
                slot[1] = 0
            if slot[1] > 0:
                self.wait(q, (slot[0], slot[1]))
            ins = self.eng[q].dma_start(out=out, in_=in_)
            self.n_ins += 1
            slot[1] += 16
            ins.then_inc(slot[0], 16)
            ev = (slot[0], slot[1], "dma_" + q)
            self.last_ev[id(slot[0])] = ev
            evs.append(ev)
        self._post(evs, reads, writes)
        return evs

    def barrier(self):
        evs = list(self.last_ev.values())
        for e in self.eng:
            for ev in evs:
                self.wait(e, ev)


class Rot:
    def __init__(self, items):
        self.items = items
        self.i = 0

    def next(self):
        it = self.items[self.i % len(self.items)]
        self.i += 1
        return it


def build_program(cfg, dbg=(), stop=None):
    c = cfg
    nc = bass.Bass("TRN2", target_bir_lowering=False)
    L = c.DEPTH
    D, S, CTX, T, DC = c.D, c.S, c.CTX, c.T, c.DC

    def din(name, shape):
        return nc.dram_tensor(name, list(shape), F32, kind="ExternalInput").ap()

    xin = din("xin", [T, D])
    cond = din("cond", [2 * DC, 128])
    vecs = din("vecs", [L, c.NV, 128])
    w_mod_a = din("w_mod_a", [L, D, c.MR])
    w_mod_b = din("w_mod_b", [L, c.MR, 6 * D])
    w_in = din("w_in", [L, D, c.DIN])
    w_uq = din("w_uq", [L, c.QL, c.H * 192])
    w_ukv = din("w_ukv", [L, c.KVL, c.H * 256])
    w_pf = din("w_pf", [L, c.FW, D])
    w_pc = din("w_pc", [L, c.CW, D])
    w_pm = din("w_pm", [L, c.MW, D])
    w_gate_a = din("w_gate_a", [L, D, c.GR])
    w_gate_b = din("w_gate_b", [L, c.GR, 3 * D])
    w_out = din("w_out", [L, D, D])
    w_ff1 = din("w_ff1", [L, D, c.DFF])
    w_ff2 = din("w_ff2", [L, c.DFF, D])
    ident_in = din("ident", [128, 128])
    perm_in = din("perm", [128, 128])
    rope_in = din("rope", [2, 128, T])
    dftc_in = din("dftc", [c.FW, 2 * c.FW])
    dftp_in = din("dftp", [2 * S, S])
    dftx_in = din("dftx", [2 * CTX, CTX])
    y_out = nc.dram_tensor("y", [S, D], F32, kind="ExternalOutput").ap()

    def scratch(name, shape, dt=BF16):
        if name in dbg:
            return nc.dram_tensor(name, list(shape), dt, kind="ExternalOutput").ap()
        return nc.dram_tensor(name, list(shape), dt).ap()

    R = scratch("R", [T, D], F32)
    HT = scratch("HT", [D, T])
    UT = scratch("UT", [c.DINP + c.GR, T])
    VT = scratch("VT", [c.CW, T], F32)
    CQG = scratch("CQG", [c.QL, T])
    CKVG = scratch("CKVG", [c.KVL, T])
    KRT = scratch("KRT", [64, T])
    QT = scratch("QT", [c.H * 192, T])
    KNT = scratch("KNT", [c.H * 128, T])
    VV = scratch("VV", [T, c.H * 128])
    OT = scratch("OT", [c.MW, T])
    AB = scratch("AB", [T, 2 * c.FW])
    FT = scratch("FT", [c.FW, T])
    CT = scratch("CT", [c.CW, T])
    MT = scratch("MT", [D, T])
    Y = scratch("Y", [T, D], F32)
    HID = scratch("HID", [c.DFF, T])
    DFTP = scratch("DFTPb", [2 * S, S])
    DFTX = scratch("DFTXb", [2 * CTX, CTX])

    tk = TK(nc)
    nm = [0]

    def sbt(st, shape, dt, name=None):
        nm[0] += 1
        return st.enter_context(nc.sbuf_tensor("%s_%d" % (name or "t", nm[0]), list(shape), dt))

    def pst(st, shape, dt, name=None):
        nm[0] += 1
        return st.enter_context(nc.psum_tensor("%s_%d" % (name or "p", nm[0]), list(shape), dt))

    def mm(out, lhsT, rhs, start, stop):
        return lambda: nc.tensor.matmul(out, lhsT=lhsT, rhs=rhs, start=start, stop=stop)

    def fm(ap):
        return ap.rearrange("(kc p) t -> p kc t", p=128)

    def split_k(kcn, step=8):
        return [(k0, min(step, kcn - k0)) for k0 in range(0, kcn, step)]

    P = tk.stack
    idf = sbt(P, [128, 128], F32, "idf")
    idb = sbt(P, [128, 128], BF16, "idb")
    ones_f = sbt(P, [128, 128], F32, "ones_f")
    ones_b = sbt(P, [128, 128], BF16, "ones_b")
    permb = sbt(P, [128, 128], BF16, "permb")
    condT = sbt(P, [128, 2 * DC], BF16, "condT")
    vecT = sbt(P, [128, c.NV], F32, "vecT")
    modT = sbt(P, [128, 6, 2, DC], F32, "modT")
    gsa = sbt(P, [128, 2, DC], F32, "gsa")
    gsm = sbt(P, [128, 2, DC], F32, "gsm")
    gpa = sbt(P, [128, 2, DC], F32, "gpa")
    gpm = sbt(P, [128, 2, DC], F32, "gpm")
    rstd_q = sbt(P, [128, T], F32, "rstd_q")
    rstd_kv = sbt(P, [128, T], F32, "rstd_kv")
    rstd_kv_tm = sbt(P, [128, c.NTT], F32, "rstd_kv_tm")
    Bc = Buf("consts")
    Bvec = Buf("vecT")
    Bmod = Buf("mod")
    Brq = Buf("rstd_q")
    Brkv = Buf("rstd_kv")

    def vcol(name, j=0, n=1):
        o, _ = c.V[name]
        return vecT[:, o + j:o + j + n]

    def gemm(orient, kgroups, a_src, tblocks, nblocks, epi, st, banks, a_slots=2, w_slots=2):
        kmax = max(k for _, k in kgroups)
        wmax = max(nb["width"] for nb in nblocks)
        tmax = max(tw for _, tw in tblocks)
        wr = Rot([(sbt(st, [128, kmax, wmax], BF16, "W"), Buf("W")) for _ in range(w_slots)])
        ar = Rot([(sbt(st, [128, kmax, tmax], BF16, "A"), Buf("A")) for _ in range(a_slots)])
        for nbi, nb in enumerate(nblocks):
            for kgi, (kc0, kcn) in enumerate(kgroups):
                W, Wb = wr.next()
                tk.dma("pool", [(W[:, ko:ko + kn, co:co + cw], ap) for (ko, kn, co, cw, ap) in nb["load"](kc0, kcn)],
                       writes=[Wb])
                for tbi, (t0, tw) in enumerate(tblocks):
                    A, Ab = ar.next()
                    tk.dma("sp", [(A[:, ko:ko + kn, 0:tw], ap) for (ko, kn, ap) in a_src(kc0, kcn, t0, tw)],
                           writes=[Ab])
                    info = dict(nb=nbi, kg=kgi, nkg=len(kgroups), tb=tbi, t0=t0, tw=tw)
                    if orient == "FM":
                        for si, (off, m, tag) in enumerate(nb["subs"]):
                            pt, pb = banks.next()
                            tk.group("pe", [mm(pt[0:m, 0:tw], W[:, k, off:off + m], A[:, k, 0:tw], k == 0, k == kcn - 1)
                                            for k in range(kcn)], reads=[Wb, Ab], writes=[pb])
                            epi(pt, pb, m, tw, dict(info, si=si, tag=tag))
                    else:
                        for tt in range(tw // 128):
                            for si, (off, n, tag) in enumerate(nb["subs"]):
                                pt, pb = banks.next()
                                tk.group("pe", [mm(pt[:, 0:n], A[:, k, tt * 128:(tt + 1) * 128], W[:, k, off:off + n],
                                                   k == 0, k == kcn - 1) for k in range(kcn)],
                                         reads=[Wb, Ab], writes=[pb])
                                epi(pt, pb, 128, n, dict(info, si=si, tag=tag, tt=t0 // 128 + tt))

    def a_from(ATd, row0=0):
        v = fm(ATd)
        r0 = row0 // 128

        def f(kc0, kcn, t0, tw):
            return [(ko, kn, v[:, r0 + kc0 + ko:r0 + kc0 + ko + kn, t0:t0 + tw]) for ko, kn in split_k(kcn)]
        return f

    def w_cols(Wd, col0, width, row0=0):
        v = fm(Wd)
        r0 = row0 // 128

        def f(kc0, kcn):
            return [(ko, kn, 0, width, v[:, r0 + kc0 + ko:r0 + kc0 + ko + kn, col0:col0 + width])
                    for ko, kn in split_k(kcn)]
        return f

    def w_segs(Wd, segs):
        v = fm(Wd)

        def f(kc0, kcn):
            out = []
            co = 0
            for col0, width in segs:
                for ko, kn in split_k(kcn):
                    out.append((ko, kn, co, width, v[:, kc0 + ko:kc0 + ko + kn, col0:col0 + width]))
                co += width
            return out
        return f

    def simple_nblocks(Wd, col0, ncols, tagf, nbw=512, subw=128):
        nbl = []
        for b0 in range(0, ncols, nbw):
            w = min(nbw, ncols - b0)
            subs = [(o, min(subw, w - o), tagf(col0 + b0 + o)) for o in range(0, w, subw)]
            nbl.append(dict(width=w, load=w_cols(Wd, col0 + b0, w), subs=subs))
        return nbl

    def fp_banks(st, n, name="pb"):
        return Rot([(pst(st, [128, 512], F32, name), Buf(name)) for _ in range(n)])

    class Stager:
        def __init__(s, st, n, shape, dt):
            s.rot = Rot([(sbt(st, shape, dt, "stg"), Buf("stg")) for _ in range(n)])
            s.flip = 0

        def next(s):
            return s.rot.next()

    def evac_copy(eng, out_ap, in_ap, reads, writes):
        if eng == "act":
            return tk.op("act", lambda: nc.scalar.copy(out=out_ap, in_=in_ap), reads=reads, writes=writes)
        return tk.op("dve", lambda: nc.vector.tensor_copy(out=out_ap, in_=in_ap), reads=reads, writes=writes)

    flip = [0]

    def alt():
        flip[0] ^= 1
        return "act" if flip[0] else "dve"

    def rsqrt_ops(out_ap, in_ap, scale, reads, writes):
        tk.op("act", lambda: nc.scalar.activation(out=out_ap, in_=in_ap, func=AF.Sqrt, scale=scale, bias=EPS),
              reads=reads, writes=writes)
        tk.op("dve", lambda: nc.vector.reciprocal(out=out_ap, in_=out_ap), reads=writes, writes=writes)

    with ExitStack() as st:
        ps = pst(st, [128, 512], F32)
        pb = Buf()
        tmp = sbt(st, [128, 128], F32)
        tb_ = Buf()
        tk.dma("sp", [(idf[:], ident_in)], writes=[Bc])
        tk.dma("pool", [(permb[:], perm_in)], writes=[Bc])
        tk.op("dve", lambda: nc.vector.tensor_copy(out=idb[:], in_=idf[:]), reads=[Bc], writes=[Bc])
        tk.op("dve", lambda: nc.vector.memset(ones_f[:], 1.0), writes=[Bc])
        tk.op("dve", lambda: nc.vector.memset(ones_b[:], 1.0), writes=[Bc])
        tk.dma("sp", [(tmp[0:2 * DC, :], cond)], writes=[tb_])
        tk.op("pe", lambda: nc.tensor.transpose(ps[:, 0:2 * DC], tmp[0:2 * DC, :], idf[0:2 * DC, 0:2 * DC]),
              reads=[tb_, Bc], writes=[pb])
        tk.op("act", lambda: nc.scalar.activation(out=condT[:], in_=ps[:, 0:2 * DC], func=AF.Silu),
              reads=[pb], writes=[Bc])
        rows = 256
        tk.dma("sp", [(R[r0:r0 + rows, :], xin[r0:r0 + rows, :]) for r0 in range(0, T, rows)])
        tk.dma("pool", [(DFTP[r0:r0 + 128, :], dftp_in[r0:r0 + 128, :]) for r0 in range(0, 2 * S, 128)])
        tk.dma("pool", [(DFTX[r0:r0 + 128, :], dftx_in[r0:r0 + 128, :]) for r0 in range(0, 2 * CTX, 128)])
    tk.barrier()

    def phase_vectors(l):
        with ExitStack() as st:
            ps = pst(st, [128, 512], F32)
            pb = Buf()
            rin = Rot([(sbt(st, [128, 128], F32), Buf()) for _ in range(2)])
            for r0 in range(0, c.NV, 128):
                n = min(128, c.NV - r0)
                t_, b_ = rin.next()
                tk.dma("sp", [(t_[0:n, :], vecs[l, r0:r0 + n, :])], writes=[b_])
                tk.op("pe", lambda: nc.tensor.transpose(ps[:, 0:n], t_[0:n, :], idf[0:n, 0:n]),
                      reads=[b_, Bc], writes=[pb])
                tk.op("dve", lambda: nc.vector.tensor_copy(out=vecT[:, r0:r0 + n], in_=ps[:, 0:n]),
                      reads=[pb], writes=[Bvec])
            wa = sbt(st, [128, DC, c.MR], BF16)
            wab = Buf()
            tk.dma("pool", [(wa[:, ko:ko + kn, :], fm(w_mod_a[l])[:, ko:ko + kn, :]) for ko, kn in split_k(DC)],
                   writes=[wab])
            m1 = sbt(st, [128, 2, 2], BF16)
            m1b = Buf()
            condv = condT[:].rearrange("p (two j) -> p j two", two=2)
            for r in range(2):
                tk.group("pe", [mm(ps[:, 2 * r:2 * r + 2], wa[:, j, r * 128:(r + 1) * 128], condv[:, j, :], j == 0, j == DC - 1)
                                for j in range(DC)], reads=[wab, Bc], writes=[pb])
            tk.op("dve", lambda: nc.vector.tensor_copy(out=m1[:].rearrange("p a b -> p (a b)"), in_=ps[:, 0:4]),
                  reads=[pb], writes=[m1b])
            wbr = Rot([(sbt(st, [128, 2, D], BF16), Buf()) for _ in range(2)])
            bo, _ = c.V["b_mod"]
            for j6 in range(6):
                wb, wbb = wbr.next()
                tk.dma("pool", [(wb[:, :, :], fm(w_mod_b[l])[:, :, j6 * D:(j6 + 1) * D])], writes=[wbb])
                fns = []
                for cc in range(DC):
                    for r in range(2):
                        fns.append(mm(ps[:, 2 * cc:2 * cc + 2], wb[:, r, cc * 128:(cc + 1) * 128], m1[:, r, :], r == 0, r == 1))
                tk.group("pe", fns, reads=[wbb, m1b], writes=[pb])
                pv = ps[:, 0:2 * DC].rearrange("p (c two) -> p c two", two=2)
                for cd in range(2):
                    tk.op("dve", lambda cd=cd: nc.vector.tensor_tensor(
                        out=modT[:, j6, cd, :], in0=pv[:, :, cd], in1=vecT[:, bo + j6 * DC:bo + (j6 + 1) * DC], op=ALU.add),
                        reads=[pb, Bvec], writes=[Bmod])
            for cd in range(2):
                for (dst, sc_i, gpre) in ((gsa, 1, "g_mix_pre"), (gsm, 4, "g_mlp_pre")):
                    tk.op("dve", lambda dst=dst, sc_i=sc_i, gpre=gpre, cd=cd: nc.vector.scalar_tensor_tensor(
                        out=dst[:, cd, :], in0=modT[:, sc_i, cd, :], scalar=1.0, in1=vcol(gpre, 0, DC),
                        op0=ALU.add, op1=ALU.mult), reads=[Bmod, Bvec], writes=[Bmod])
                for (dst, g_i, gpost) in ((gpa, 2, "g_mix_post"), (gpm, 5, "g_mlp_post")):
                    tk.op("dve", lambda dst=dst, g_i=g_i, gpost=gpost, cd=cd: nc.vector.tensor_tensor(
                        out=dst[:, cd, :], in0=modT[:, g_i, cd, :], in1=vcol(gpost, 0, DC), op=ALU.mult),
                        reads=[Bmod, Bvec], writes=[Bmod])
        tk.barrier()

    def modulate_tiles(st, tiles_src, gs, sh_idx, lat_only):
        raise NotImplementedError

    def phase_modulate(l, gs, sh_idx, lat_only=False, resid=None):
        with ExitStack() as st:
            xr = Rot([(sbt(st, [128, D], F32, "x"), Buf()) for _ in range(2)])
            xnr = Rot([(sbt(st, [128, D], BF16, "xn"), Buf()) for _ in range(1)])
            sq = sbt(st, [128, D], BF16, "sq")
            sqb = Buf()
            ssr = Rot([(sbt(st, [128, 4], F32, "ss"), Buf()) for _ in range(2)])
            hr = Rot([(sbt(st, [128, DC, c.TB], BF16, "hT"), Buf()) for _ in range(1)])
            ptr = Rot([(pst(st, [128, 8, 128], BF16, "pT"), Buf()) for _ in range(2)])
            if resid is not None:
                yr = Rot([(sbt(st, [128, D], F32, "y"), Buf()) for _ in range(2)])
                gpbc = sbt(st, [128, 2, D], F32, "gpbc")
                gpb = Buf()
                pbc = pst(st, [128, 512], F32)
                pbcb = Buf()
                dg = Rot([(sbt(st, [128, 128], F32, "dg"), Buf()) for _ in range(2)])
                for cd in range(1 if lat_only else 2):
                    for c0 in range(0, DC, 4):
                        for j in range(min(4, DC - c0)):
                            d_, db_ = dg.next()
                            tk.op("dve", lambda d_=d_, j=j: nc.vector.tensor_scalar(
                                out=d_[:], in0=idf[:], scalar1=resid[:, cd, c0 + j:c0 + j + 1], scalar2=None,
                                op0=ALU.mult), reads=[Bc, Bmod], writes=[db_])
                            tk.op("pe", lambda d_=d_, j=j: nc.tensor.matmul(
                                pbc[:, j * 128:(j + 1) * 128], lhsT=ones_f[:], rhs=d_[:], start=True, stop=True),
                                reads=[db_, Bc], writes=[pbcb])
                        n = min(4, DC - c0) * 128
                        tk.op("act", lambda n=n, c0=c0: nc.scalar.copy(out=gpbc[:, cd, c0 * 128:c0 * 128 + n], in_=pbc[:, 0:n]),
                              reads=[pbcb], writes=[gpb])
            for (t0, tw) in c.tblocks(lat_only):
                cd = 0 if t0 < S else 1
                hT, hb = hr.next()
                for tt in range(tw // 128):
                    r0 = t0 + tt * 128
                    x_, xb = xr.next()
                    ss, ssb = ssr.next()
                    tk.dma("sp", [(x_[:], R[r0:r0 + 128, :])], writes=[xb])
                    if resid is not None:
                        y_, yb = yr.next()
                        tk.dma("sp", [(y_[:], Y[r0:r0 + 128, :])], writes=[yb])
                        tk.op("act", lambda: nc.scalar.activation(out=sq[:], in_=y_[:], func=AF.Square, accum_out=ss[:, 0:1]),
                              reads=[yb], writes=[sqb, ssb])
                        rsqrt_ops(ss[:, 1:2], ss[:, 0:1], 1.0 / D, [ssb], [ssb])
                        tk.op("dve", lambda: nc.vector.scalar_tensor_tensor(
                            out=y_[:], in0=y_[:], scalar=ss[:, 1:2], in1=gpbc[:, cd, :], op0=ALU.mult, op1=ALU.mult),
                            reads=[ssb, gpb, yb], writes=[yb])
                        tk.op("pool", lambda: nc.gpsimd.tensor_tensor(out=x_[:], in0=x_[:], in1=y_[:], op=ALU.add),
                              reads=[xb, yb], writes=[xb])
                        tk.dma("act", [(R[r0:r0 + 128, :], x_[:])], reads=[xb])
                    if gs is None:
                        continue
                    tk.op("act", lambda: nc.scalar.activation(out=sq[:], in_=x_[:], func=AF.Square, accum_out=ss[:, 2:3]),
                          reads=[xb], writes=[sqb, ssb])
                    rsqrt_ops(ss[:, 3:4], ss[:, 2:3], 1.0 / D, [ssb], [ssb])
                    xn, xnb = xnr.next()
                    tk.op("dve", lambda: nc.vector.tensor_scalar(out=xn[:], in0=x_[:], scalar1=ss[:, 3:4], scalar2=None,
                                                                 op0=ALU.mult), reads=[xb, ssb], writes=[xnb])
                    for c0 in range(0, DC, 8):
                        n = min(8, DC - c0)
                        pT, pTb = ptr.next()
                        tk.group("pe", [(lambda j=j: nc.tensor.transpose(pT[:, j, :], xn[:, (c0 + j) * 128:(c0 + j + 1) * 128], idb[:]))
                                        for j in range(n)], reads=[xnb, Bc], writes=[pTb])
                        for j in range(n):
                            cc = c0 + j
                            if True:
                                tk.op("act", lambda j=j, cc=cc: nc.scalar.activation(
                                    out=hT[:, cc, tt * 128:(tt + 1) * 128], in_=pT[:, j, :], func=AF.Identity,
                                    scale=gs[:, cd, cc:cc + 1], bias=modT[:, sh_idx, cd, cc:cc + 1]),
                                    reads=[pTb, Bmod], writes=[hb])
                            else:
                                tk.op("dve", lambda j=j, cc=cc: nc.vector.tensor_scalar(
                                    out=hT[:, cc, tt * 128:(tt + 1) * 128], in0=pT[:, j, :],
                                    scalar1=gs[:, cd, cc:cc + 1], scalar2=modT[:, sh_idx, cd, cc:cc + 1],
                                    op0=ALU.mult, op1=ALU.add), reads=[pTb, Bmod], writes=[hb])
                if gs is not None:
                    tk.dma("act", [(fm(HT)[:, ko:ko + kn, t0:t0 + tw], hT[:, ko:ko + kn, 0:tw]) for ko, kn in split_k(DC)],
                           reads=[hb])
        tk.barrier()

    def store_fm(stg, dst_rows_fn):
        def epi(pt, pb, m, tw, info):
            s_, sb_ = stg.next()
            evac_copy(alt(), s_[0:m, 0:tw], pt[0:m, 0:tw], [pb], [sb_])
            dst = dst_rows_fn(info)
            tk.dma("act", [(dst[:, info["t0"]:info["t0"] + tw], s_[0:m, 0:tw])], reads=[sb_])
        return epi

    def phase_win(l):
        with ExitStack() as st:
            banks = fp_banks(st, 6)
            stg = Stager(st, 3, [128, c.TB], BF16)
            stf = Stager(st, 2, [128, c.TB], F32)
            sgr = Stager(st, 2, [128, c.TB], F32)
            nbl = []
            nbl += simple_nblocks(w_in[l], 0, c.FW, lambda col: ("u", col))
            for j0 in range(0, c.CW // 128, 2):
                nj = min(2, c.CW // 128 - j0)
                segs = [(c.OFF_C + j0 * 128, nj * 128), (c.OFF_C + c.CW + j0 * 128, nj * 128)]
                subs = []
                for j in range(nj):
                    subs.append((j * 128, 128, ("a", j0 + j)))
                    subs.append((nj * 128 + j * 128, 128, ("g", j0 + j)))
                nbl.append(dict(width=2 * nj * 128, load=w_segs(w_in[l], segs), subs=subs))
            nbl += simple_nblocks(w_in[l], c.OFF_Q, c.DIN - c.OFF_Q, lambda col: ("u", col))
            nbl += simple_nblocks(w_gate_a[l], 0, c.GR, lambda col: ("u", c.DINP + col))
            held = {}

            def epi(pt, pb, m, tw, info):
                kind, v = info["tag"]
                t0 = info["t0"]
                if kind == "u":
                    s_, sb_ = stg.next()
                    evac_copy(alt(), s_[0:m, 0:tw], pt[0:m, 0:tw], [pb], [sb_])
                    tk.dma("act", [(UT[v:v + m, t0:t0 + tw], s_[0:m, 0:tw])], reads=[sb_])
                elif kind == "a":
                    held["a"] = (pt, pb)
                else:
                    apt, apb = held.pop("a")
                    sg, sgb = sgr.next()
                    tk.op("act", lambda: nc.scalar.activation(out=sg[:, 0:tw], in_=pt[:, 0:tw], func=AF.Sigmoid),
                          reads=[pb], writes=[sgb])
                    s_, sb_ = stf.next()
                    tk.op("dve", lambda: nc.vector.tensor_tensor(out=s_[:, 0:tw], in0=apt[:, 0:tw], in1=sg[:, 0:tw], op=ALU.mult),
                          reads=[apb, sgb], writes=[sb_])
                    tk.dma("act", [(VT[v * 128:(v + 1) * 128, t0:t0 + tw], s_[:, 0:tw])], reads=[sb_])
            gemm("FM", [(0, DC)], a_from(HT), c.tblocks(), nbl, epi, st, banks)
        tk.barrier()

    def phase_stats(l):
        with ExitStack() as st:
            ps1 = pst(st, [128, 512], F32)
            ps1b = Buf()
            ps2 = pst(st, [128, 512], F32)
            ps2b = Buf()
            pr = pst(st, [128, 512], F32)
            prb = Buf()
            rope = sbt(st, [64, 2, T], F32)
            rpb = Buf()
            tk.dma("sp", [(rope[:, 0, :], rope_in[0, 0:64, :]), (rope[:, 1, :], rope_in[1, 0:64, :])], writes=[rpb])
            for (row0, K, gname, CG, rst, rb) in ((c.OFF_Q, c.QL, "q_norm_g", CQG, rstd_q, Brq),
                                                  (c.OFF_KV, c.KVL, "kv_norm_g", CKVG, rstd_kv, Brkv)):
                kc = K // 128
                st2 = st.enter_context(ExitStack()) if False else ExitStack()
                xr = Rot([(sbt(st2, [128, kc, c.TB], BF16), Buf()) for _ in range(2)])
                sqr = Rot([(sbt(st2, [128, kc, c.TB], BF16), Buf()) for _ in range(2)])
                gr = Rot([(sbt(st2, [128, kc, c.TB], BF16), Buf()) for _ in range(2)])
                for (t0, tw) in c.tblocks():
                    x_, xb = xr.next()
                    s_, sb_ = sqr.next()
                    g_, gb = gr.next()
                    tk.dma("sp", [(x_[:, :, 0:tw], fm(UT)[:, row0 // 128:row0 // 128 + kc, t0:t0 + tw])], writes=[xb])
                    tk.op("act", lambda: nc.scalar.activation(out=s_[:, :, 0:tw], in_=x_[:, :, 0:tw], func=AF.Square),
                          reads=[xb], writes=[sb_])
                    tk.group("pe", [mm(ps1[:, 0:tw], ones_b[:], s_[:, k, 0:tw], k == 0, k == kc - 1) for k in range(kc)],
                             reads=[sb_, Bc], writes=[ps1b])
                    rsqrt_ops(rst[:, t0:t0 + tw], ps1[:, 0:tw], 1.0 / K, [ps1b], [rb])
                    if K == c.KVL:
                        for tt in range(tw // 128):
                            tk.group("pe", [mm(ps2[:, 0:1], s_[:, k, tt * 128:(tt + 1) * 128], ones_b[:, 0:1], k == 0, k == kc - 1)
                                            for k in range(kc)], reads=[sb_, Bc], writes=[ps2b])
                            ti = t0 // 128 + tt
                            rsqrt_ops(rstd_kv_tm[:, ti:ti + 1], ps2[:, 0:1], 1.0 / K, [ps2b], [rb])
                    for k in range(kc):
                        tk.op("dve", lambda k=k: nc.vector.tensor_scalar(
                            out=g_[:, k, 0:tw], in0=x_[:, k, 0:tw], scalar1=vcol(gname, k), scalar2=None, op0=ALU.mult),
                            reads=[xb, Bvec], writes=[gb])
                    tk.dma("act", [(fm(CG)[:, :, t0:t0 + tw], g_[:, :, 0:tw])], reads=[gb])
                tk.barrier()
                st2.close()
            kr0 = c.OFF_KV + c.KVL
            xr = Rot([(sbt(st, [64, c.TB], BF16), Buf()) for _ in range(2)])
            t1r = Rot([(sbt(st, [64, c.TB], F32), Buf()) for _ in range(2)])
            orr = Rot([(sbt(st, [64, c.TB], BF16), Buf()) for _ in range(2)])
            for (t0, tw) in c.tblocks():
                x_, xb = xr.next()
                t1, t1b = t1r.next()
                o_, ob = orr.next()
                tk.dma("sp", [(x_[:, 0:tw], UT[kr0:kr0 + 64, t0:t0 + tw])], writes=[xb])
                tk.op("pe", lambda: nc.tensor.matmul(pr[0:64, 0:tw], lhsT=permb[0:64, 0:64], rhs=x_[:, 0:tw], start=True, stop=True),
                      reads=[xb, Bc], writes=[prb])
                tk.op("dve", lambda: nc.vector.tensor_tensor(out=t1[:, 0:tw], in0=x_[:, 0:tw], in1=rope[:, 0, t0:t0 + tw], op=ALU.mult),
                      reads=[xb, rpb], writes=[t1b])
                tk.op("dve", lambda: nc.vector.tensor_tensor(out=o_[:, 0:tw], in0=pr[0:64, 0:tw], in1=rope[:, 1, t0:t0 + tw], op=ALU.mult),
                      reads=[prb, rpb], writes=[ob])
                tk.op("dve", lambda: nc.vector.tensor_tensor(out=o_[:, 0:tw], in0=o_[:, 0:tw], in1=t1[:, 0:tw], op=ALU.add),
                      reads=[ob, t1b], writes=[ob])
                tk.dma("act", [(KRT[:, t0:t0 + tw], o_[:, 0:tw])], reads=[ob])
        tk.barrier()

    def phase_q(l, lat_only):
        with ExitStack() as st:
            banks = fp_banks(st, 5)
            pr = pst(st, [128, 512], F32)
            prb = Buf()
            stg = Stager(st, 3, [128, c.TB], BF16)
            xrr = Rot([(sbt(st, [128, c.TB], BF16), Buf()) for _ in range(2)])
            t1r = Rot([(sbt(st, [128, c.TB], F32), Buf()) for _ in range(2)])
            rope = sbt(st, [128, 2, T], F32)
            rpb = Buf()
            tk.dma("sp", [(rope[:, 0, :], rope_in[0]), (rope[:, 1, :], rope_in[1])], writes=[rpb])
            nbl = []
            for h0 in range(0, c.H, 4):
                nh = min(4, c.H - h0)
                nbl.append(dict(width=nh * 128, load=w_segs(w_uq[l], [((h0 + i) * 192, 128) for i in range(nh)]),
                                subs=[(i * 128, 128, ("n", h0 + i)) for i in range(nh)]))
            for h0 in range(0, c.H, 8):
                nh = min(8, c.H - h0)
                nbl.append(dict(width=nh * 64, load=w_segs(w_uq[l], [((h0 + i) * 192 + 128, 64) for i in range(nh)]),
                                subs=[(j * 128, 128, ("r", h0 + 2 * j)) for j in range(nh // 2)]))
            QR0 = c.H * 128

            def epi(pt, pb, m, tw, info):
                kind, h = info["tag"]
                t0 = info["t0"]
                if kind == "n":
                    s_, sb_ = stg.next()
                    tk.op("dve", lambda: nc.vector.tensor_tensor(out=s_[:, 0:tw], in0=pt[:, 0:tw], in1=rstd_q[:, t0:t0 + tw], op=ALU.mult),
                          reads=[pb, Brq], writes=[sb_])
                    tk.dma("act", [(QT[h * 128:(h + 1) * 128, t0:t0 + tw], s_[:, 0:tw])], reads=[sb_])
                else:
                    x_, xb = xrr.next()
                    t1, t1b = t1r.next()
                    s_, sb_ = stg.next()
                    tk.op("dve", lambda: nc.vector.tensor_tensor(out=x_[:, 0:tw], in0=pt[:, 0:tw], in1=rstd_q[:, t0:t0 + tw], op=ALU.mult),
                          reads=[pb, Brq], writes=[xb])
                    tk.op("pe", lambda: nc.tensor.matmul(pr[:, 0:tw], lhsT=permb[:], rhs=x_[:, 0:tw], start=True, stop=True),
                          reads=[xb, Bc], writes=[prb])
                    tk.op("dve", lambda: nc.vector.tensor_tensor(out=t1[:, 0:tw], in0=x_[:, 0:tw], in1=rope[:, 0, t0:t0 + tw], op=ALU.mult),
                          reads=[xb, rpb], writes=[t1b])
                    tk.op("dve", lambda: nc.vector.tensor_tensor(out=s_[:, 0:tw], in0=pr[:, 0:tw], in1=rope[:, 1, t0:t0 + tw], op=ALU.mult),
                          reads=[prb, rpb], writes=[sb_])
                    tk.op("dve", lambda: nc.vector.tensor_tensor(out=s_[:, 0:tw], in0=s_[:, 0:tw], in1=t1[:, 0:tw], op=ALU.add),
                          reads=[sb_, t1b], writes=[sb_])
                    tk.dma("act", [(QT[QR0 + h * 64:QR0 + h * 64 + 128, t0:t0 + tw], s_[:, 0:tw])], reads=[sb_])
            gemm("FM", [(0, c.QL // 128)], a_from(CQG), c.tblocks(lat_only), nbl, epi, st, banks)
        tk.barrier()

    def phase_kv(l):
        with ExitStack() as st:
            banks = fp_banks(st, 6)
            stg = Stager(st, 3, [128, 512], BF16)
            nbl = []
            for h0 in range(0, c.H, 4):
                nh = min(4, c.H - h0)
                segs = [((h0 + i) * 256, 128) for i in range(nh)]
                nbl.append(dict(width=nh * 128, load=w_segs(w_ukv[l], segs),
                                subs=[(i * 128, 128, h0 + i) for i in range(nh)]))

            def epi_k(pt, pb, m, tw, info):
                h = info["tag"]
                t0 = info["t0"]
                s_, sb_ = stg.next()
                tk.op("dve", lambda: nc.vector.tensor_tensor(out=s_[:, 0:tw], in0=pt[:, 0:tw], in1=rstd_kv[:, t0:t0 + tw], op=ALU.mult),
                      reads=[pb, Brkv], writes=[sb_])
                tk.dma("act", [(KNT[h * 128:(h + 1) * 128, t0:t0 + tw], s_[:, 0:tw])], reads=[sb_])
            gemm("FM", [(0, c.KVL // 128)], a_from(CKVG), c.tblocks(), nbl, epi_k, st, banks)
            if os.environ.get("KVHALF"):
                tk.barrier()
                return
            nbl = []
            for h0 in range(0, c.H, 4):
                nh = min(4, c.H - h0)
                segs = [((h0 + i) * 256 + 128, 128) for i in range(nh)]
                nbl.append(dict(width=nh * 128, load=w_segs(w_ukv[l], segs), subs=[(0, nh * 128, h0)]))

            def epi_v(pt, pb, m, n, info):
                h0 = info["tag"]
                ti = info["tt"]
                s_, sb_ = stg.next()
                if os.environ.get("VCOPY"):
                    evac_copy("dve", s_[:, 0:n], pt[:, 0:n], [pb], [sb_])
                else:
                    tk.op("act", lambda: nc.scalar.activation(out=s_[:, 0:n], in_=pt[:, 0:n], func=AF.Identity,
                                                              scale=rstd_kv_tm[:, ti:ti + 1]), reads=[pb, Brkv], writes=[sb_])
                tk.dma("act", [(VV[ti * 128:(ti + 1) * 128, h0 * 128:h0 * 128 + n], s_[:, 0:n])], reads=[sb_])
            gemm("TM", [(0, c.KVL // 128)], a_from(CKVG), c.tblocks(), nbl, epi_v, st, banks)
        tk.barrier()

    def phase_attn(l, lat_only, pump=None):
        with ExitStack() as st:
            sbk = Rot([(pst(st, [128, 512], F32, "ps_s"), Buf()) for _ in range(4)])
            oacc = Rot([(pst(st, [128, 512], F32, "ps_o"), Buf()) for _ in range(2)])
            racc = Rot([(pst(st, [128, 512], F32, "ps_r"), Buf()) for _ in range(2)])
            TQ = S if lat_only else T
            kr = sbt(st, [128, T], BF16)
            krb = Buf()
            tk.op("dve", lambda: nc.vector.memset(kr[64:128, :], 0.0), writes=[krb])
            tk.dma("sp", [(kr[0:64, :], KRT)], writes=[krb])
            knr = Rot([(sbt(st, [128, T], BF16, "kn"), Buf()) for _ in range(2)])
            vr = Rot([(sbt(st, [128, c.NTT, 128], BF16, "v"), Buf()) for _ in range(2)])
            qnr = Rot([(sbt(st, [128, T], BF16, "qn"), Buf()) for _ in range(2)])
            qrr = Rot([(sbt(st, [128, T], BF16, "qr"), Buf()) for _ in range(2)])
            for (qr_, qrb_) in qrr.items:
                tk.op("dve", lambda qr_=qr_: nc.vector.memset(qr_[64:128, :], 0.0), writes=[qrb_])
            ptr = Rot([(sbt(st, [128, c.TB], BF16, "pT"), Buf()) for _ in range(5)])
            rir = Rot([(sbt(st, [128, c.TB], F32, "ri"), Buf()) for _ in range(2)])
            stg = Stager(st, 2, [128, c.TB], BF16)
            LAG = 3
            pending = []

            def emit_pv(it):
                tw = it["tw"]
                tk.group("pe", [mm(it["po"][:, 0:tw], it["v"][:, it["kt"], :], it["pT"][:, 0:tw], it["i"] == 0, it["i"] == it["nk"] - 1),
                                mm(it["prs"][:, 0:tw], ones_b[:], it["pT"][:, 0:tw], it["i"] == 0, it["i"] == it["nk"] - 1)],
                         reads=[it["pTb"], it["vb"], Bc], writes=[it["pob"], it["prsb"]], own_acc=(it["i"] > 0))
                if it["i"] == it["nk"] - 1:
                    ri, rib = rir.next()
                    tk.op("dve", lambda: nc.vector.reciprocal(out=ri[:, 0:tw], in_=it["prs"][:, 0:tw]), reads=[it["prsb"]], writes=[rib])
                    s_, sb_ = stg.next()
                    tk.op("dve", lambda: nc.vector.tensor_tensor(out=s_[:, 0:tw], in0=it["po"][:, 0:tw], in1=ri[:, 0:tw], op=ALU.mult),
                          reads=[it["pob"], rib], writes=[sb_])
                    h, t0 = it["h"], it["t0"]
                    tk.dma("act", [(OT[h * 128:(h + 1) * 128, t0:t0 + tw], s_[:, 0:tw])], reads=[sb_])
            for h in range(c.H):
                kn, knb = knr.next()
                v_, vb = vr.next()
                qn, qnb = qnr.next()
                qr, qrb = qrr.next()
                tk.dma("sp", [(kn[:], KNT[h * 128:(h + 1) * 128, :])], writes=[knb])
                tk.dma("sp", [(v_[:], VV.rearrange("(tt p) n -> p tt n", p=128)[:, :, h * 128:(h + 1) * 128])], writes=[vb])
                tk.dma("sp", [(qn[:, 0:TQ], QT[h * 128:(h + 1) * 128, 0:TQ])], writes=[qnb])
                tk.dma("sp", [(qr[0:64, 0:TQ], QT[c.H * 128 + h * 64:c.H * 128 + (h + 1) * 64, 0:TQ])], reads=[qrb], writes=[qrb])
                for (t0, tw) in c.tblocks(lat_only):
                    kts = list(range(c.NTT)) if t0 < S else list(range(S // 128, c.NTT))
                    po, pob = oacc.next()
                    prs, prsb = racc.next()
                    nk = len(kts)
                    for i, kt in enumerate(kts):
                        pss, pssb = sbk.next()
                        tk.group("pe", [mm(pss[:, 0:tw], kn[:, kt * 128:(kt + 1) * 128], qn[:, t0:t0 + tw], True, False),
                                        mm(pss[:, 0:tw], kr[:, kt * 128:(kt + 1) * 128], qr[:, t0:t0 + tw], False, True)],
                                 reads=[knb, qnb, krb, qrb], writes=[pssb])
                        pT, pTb = ptr.next()
                        tk.op("act", lambda: nc.scalar.activation(out=pT[:, 0:tw], in_=pss[:, 0:tw], func=AF.Exp, scale=c.ATTN_SCALE),
                              reads=[pssb], writes=[pTb])
                        pending.append(dict(pT=pT, pTb=pTb, kt=kt, i=i, nk=nk, po=po, pob=pob, prs=prs, prsb=prsb,
                                            v=v_, vb=vb, tw=tw, h=h, t0=t0))
                        if len(pending) > LAG:
                            emit_pv(pending.pop(0))
                        if pump is not None:
                            pump()
            while pending:
                emit_pv(pending.pop(0))
            if pump is not None:
                pump(True)
        tk.barrier()

    def phase_fourier(l, lat_only):
        with ExitStack() as st:
            banks = fp_banks(st, 6)
            stg = Stager(st, 3, [128, 512], BF16)
            nbl = [dict(width=512, load=w_cols(dftc_in, b0, 512), subs=[(0, 512, b0)]) for b0 in range(0, 2 * c.FW, 512)]

            def epi_ab(pt, pb, m, n, info):
                ti = info["tt"]
                s_, sb_ = stg.next()
                evac_copy(alt(), s_[:, 0:n], pt[:, 0:n], [pb], [sb_])
                tk.dma("act", [(AB[ti * 128:(ti + 1) * 128, info["tag"]:info["tag"] + n], s_[:, 0:n])], reads=[sb_])
            gemm("TM", [(0, c.FW // 128)], a_from(UT), c.tblocks(lat_only), nbl, epi_ab, st, banks)
        tk.barrier()
        with ExitStack() as st:
            banks = fp_banks(st, 6)
            stg = Stager(st, 3, [128, 512], BF16)
            for (tok0, L_, DF) in (((0, S, DFTP),) if lat_only else ((0, S, DFTP), (S, CTX, DFTX))):
                lc = L_ // 128
                abv = AB.rearrange("(tt p) n -> p tt n", p=128)
                nbl = []
                for b0 in range(0, c.FW, 512):
                    w = min(512, c.FW - b0)

                    def load(kc0, kcn, b0=b0, w=w, lc=lc, tok0=tok0):
                        out = []
                        for half in range(2):
                            for ko, kn in split_k(lc):
                                out.append((half * lc + ko, kn, 0, w,
                                            abv[:, tok0 // 128 + ko:tok0 // 128 + ko + kn, half * c.FW + b0:half * c.FW + b0 + w]))
                        return out
                    nbl.append(dict(width=w, load=load, subs=[(o, 128, b0 + o) for o in range(0, w, 128)]))

                def epi_f(pt, pb, m, tw, info, tok0=tok0):
                    s_, sb_ = stg.next()
                    evac_copy(alt(), s_[:, 0:tw], pt[:, 0:tw], [pb], [sb_])
                    r = info["tag"]
                    tk.dma("act", [(FT[r:r + 128, tok0 + info["t0"]:tok0 + info["t0"] + tw], s_[:, 0:tw])], reads=[sb_])
                tbl = [(t0, min(c.TB, L_ - t0)) for t0 in range(0, L_, c.TB)]
                gemm("FM", [(0, 2 * lc)], a_from(DF), tbl, nbl, epi_f, st, banks)
        tk.barrier()

    def conv_alloc(st, lat_only):
        CWC = c.CW // 128
        PAD = CONV_K // 2
        TT_ = S if lat_only else T
        yc = sbt(st, [128, CWC, TT_], F32, "yc")
        ycb = [Buf() for _ in range(CWC)]
        vps = [(sbt(st, [128, T + 4 * PAD], F32, "vp"), Buf()) for _ in range(2)]
        return dict(yc=yc, ycb=ycb, vps=vps)

    def conv_mac_gen(l, lat_only, cs):
        CWC = c.CW // 128
        PAD = CONV_K // 2
        regions = [(0, S)] if lat_only else [(0, S), (S, CTX)]
        yc, ycb, vps = cs["yc"], cs["ycb"], cs["vps"]
        for (vp, vpb) in vps:
            tk.op("pool", lambda vp=vp: nc.gpsimd.memset(vp[:], 0.0), writes=[vpb])
        wo, _ = c.V["conv_w"]

        def load(j):
            vp, vpb = vps[j % 2]
            prs = []
            for ri, (r0, rl) in enumerate(regions):
                off = PAD + r0 + ri * 2 * PAD
                prs.append((vp[:, off:off + rl], VT[j * 128:(j + 1) * 128, r0:r0 + rl]))
            tk.dma("sp", prs, writes=[vpb])
        load(0)
        yield
        for j in range(CWC):
            if j + 1 < CWC:
                load(j + 1)
                yield
            vp, vpb = vps[j % 2]
            for ri, (r0, rl) in enumerate(regions):
                base = r0 + ri * 2 * PAD
                acc = yc[:, j, r0:r0 + rl]
                tk.op("dve", lambda: nc.vector.tensor_scalar(
                    out=acc, in0=vp[:, base:base + rl], scalar1=vecT[:, wo + j:wo + j + 1], scalar2=vcol("conv_b", j),
                    op0=ALU.mult, op1=ALU.add), reads=[vpb, Bvec], writes=[ycb[j]])
                yield
                for k in range(1, CONV_K):
                    tk.op("dve", lambda k=k: nc.vector.scalar_tensor_tensor(
                        out=acc, in0=vp[:, base + k:base + k + rl], scalar=vecT[:, wo + k * CWC + j:wo + k * CWC + j + 1],
                        in1=acc, op0=ALU.mult, op1=ALU.add), reads=[vpb, Bvec, ycb[j]], writes=[ycb[j]])
                    yield

    def phase_conv_ln(l, lat_only, cs):
        CWC = c.CW // 128
        yc, ycb = cs["yc"], cs["ycb"]
        with ExitStack() as st:
            pm = pst(st, [128, 512], F32)
            pmb = Buf()
            pq = pst(st, [128, 512], F32)
            pqb = Buf()
            sqr = Rot([(sbt(st, [128, c.TB], F32, "sq"), Buf()) for _ in range(2)])
            mean = sbt(st, [128, c.TB], F32)
            rstd = sbt(st, [128, c.TB], F32)
            msq = sbt(st, [128, c.TB], F32)
            stb = Buf()
            zr = Rot([(sbt(st, [128, c.TB], F32, "z"), Buf()) for _ in range(2)])
            stg = Stager(st, 3, [128, c.TB], BF16)
            for (t0, tw) in c.tblocks(lat_only):
                tk.group("pe", [mm(pm[:, 0:tw], ones_f[:], yc[:, j, t0:t0 + tw], j == 0, j == CWC - 1) for j in range(CWC)],
                         reads=ycb + [Bc], writes=[pmb])
                for j in range(CWC):
                    s_, sb_ = sqr.next()
                    tk.op("act", lambda j=j: nc.scalar.activation(out=s_[:, 0:tw], in_=yc[:, j, t0:t0 + tw], func=AF.Square),
                          reads=[ycb[j]], writes=[sb_])
                    tk.group("pe", [mm(pq[:, 0:tw], ones_f[:], s_[:, 0:tw], j == 0, j == CWC - 1)], reads=[sb_, Bc], writes=[pqb],
                             own_acc=(j > 0))
                tk.op("act", lambda: nc.scalar.mul(out=mean[:, 0:tw], in_=pm[:, 0:tw], mul=1.0 / c.CW), reads=[pmb], writes=[stb])
                tk.op("dve", lambda: nc.vector.tensor_tensor(out=msq[:, 0:tw], in0=mean[:, 0:tw], in1=mean[:, 0:tw], op=ALU.mult),
                      reads=[stb], writes=[stb])
                tk.op("dve", lambda: nc.vector.scalar_tensor_tensor(out=rstd[:, 0:tw], in0=pq[:, 0:tw], scalar=1.0 / c.CW, in1=msq[:, 0:tw],
                                                                    op0=ALU.mult, op1=ALU.subtract), reads=[pqb, stb], writes=[stb])
                rsqrt_ops(rstd[:, 0:tw], rstd[:, 0:tw], 1.0, [stb], [stb])
                for j in range(CWC):
                    z, zb = zr.next()
                    tk.op("dve", lambda j=j: nc.vector.tensor_tensor(out=z[:, 0:tw], in0=yc[:, j, t0:t0 + tw], in1=mean[:, 0:tw], op=ALU.subtract),
                          reads=[ycb[j], stb], writes=[zb])
                    tk.op("dve", lambda: nc.vector.tensor_tensor(out=z[:, 0:tw], in0=z[:, 0:tw], in1=rstd[:, 0:tw], op=ALU.mult),
                          reads=[zb, stb], writes=[zb])
                    s_, sb_ = stg.next()
                    tk.op("act", lambda j=j: nc.scalar.activation(out=s_[:, 0:tw], in_=z[:, 0:tw], func=AF.Silu,
                                                                  scale=vcol("conv_ln_g", j), bias=vcol("conv_ln_b", j)),
                          reads=[zb, Bvec], writes=[sb_])
                    tk.dma("act", [(CT[j * 128:(j + 1) * 128, t0:t0 + tw], s_[:, 0:tw])], reads=[sb_])
        tk.barrier()

    def phase_merge(l, lat_only):
        FWC, CWC, MWC, GRC = c.FW // 128, c.CW // 128, c.MW // 128, c.GR // 128
        KA = FWC + CWC + MWC + GRC
        NBW = 256
        with ExitStack() as st:
            banks = fp_banks(st, 8)
            wr = Rot([(sbt(st, [128, FWC + CWC + MWC + 3 * GRC, NBW], BF16, "Wm"), Buf()) for _ in range(2)])
            ar = Rot([(sbt(st, [128, KA, c.TB], BF16, "Am"), Buf()) for _ in range(2)])
            gr = Rot([(sbt(st, [128, 3, c.TB], F32, "g"), Buf()) for _ in range(2)])
            mr = Rot([(sbt(st, [128, 2, c.TB], F32, "m"), Buf()) for _ in range(2)])
            stg = Stager(st, 3, [128, c.TB], BF16)
            bgo, _ = c.V["b_gate"]
            srcs = [(FT, FWC), (CT, CWC), (OT, MWC)]
            for n0 in range(0, D, NBW):
                W, Wb = wr.next()
                prs = []
                ko = 0
                for wd, kc in ((w_pf[l], FWC), (w_pc[l], CWC), (w_pm[l], MWC)):
                    for k0, kn in split_k(kc):
                        prs.append((W[:, ko + k0:ko + k0 + kn, :], fm(wd)[:, k0:k0 + kn, n0:n0 + NBW]))
                    ko += kc
                for i in range(3):
                    prs.append((W[:, ko:ko + GRC, :], fm(w_gate_b[l])[:, :, i * D + n0:i * D + n0 + NBW]))
                    ko += GRC
                tk.dma("pool", prs, writes=[Wb])
                for (t0, tw) in c.tblocks(lat_only):
                    A, Ab = ar.next()
                    prs = []
                    ko = 0
                    for sd, kc in srcs:
                        for k0, kn in split_k(kc):
                            prs.append((A[:, ko + k0:ko + k0 + kn, 0:tw], fm(sd)[:, k0:k0 + kn, t0:t0 + tw]))
                        ko += kc
                    prs.append((A[:, ko:ko + GRC, 0:tw], fm(UT)[:, c.DINP // 128:c.DINP // 128 + GRC, t0:t0 + tw]))
                    tk.dma("sp", prs, writes=[Ab])
                    for cs in range(NBW // 128):
                        cc = n0 // 128 + cs
                        cols = slice(cs * 128, (cs + 1) * 128)
                        ys = []
                        ko = 0
                        wko = 0
                        for kc in (FWC, CWC, MWC):
                            pt, pb = banks.next()
                            tk.group("pe", [mm(pt[:, 0:tw], W[:, wko + k, cols], A[:, ko + k, 0:tw], k == 0, k == kc - 1)
                                            for k in range(kc)], reads=[Wb, Ab], writes=[pb])
                            ys.append((pt, pb))
                            ko += kc
                            wko += kc
                        g_, gb = gr.next()
                        for i in range(3):
                            pt, pb = banks.next()
                            tk.group("pe", [mm(pt[:, 0:tw], W[:, wko + k, cols], A[:, ko + k, 0:tw], k == 0, k == GRC - 1)
                                            for k in range(GRC)], reads=[Wb, Ab], writes=[pb])
                            wko += GRC
                            tk.op("act", lambda i=i, pt=pt: nc.scalar.activation(
                                out=g_[:, i, 0:tw], in_=pt[:, 0:tw], func=AF.Sigmoid,
                                bias=vecT[:, bgo + i * DC + cc:bgo + i * DC + cc + 1], scale=1.0),
                                reads=[pb, Bvec], writes=[gb])
                        m_, mb = mr.next()
                        tk.op("dve", lambda: nc.vector.tensor_tensor(out=m_[:, 0, 0:tw], in0=ys[0][0][:, 0:tw], in1=g_[:, 0, 0:tw], op=ALU.mult),
                              reads=[ys[0][1], gb], writes=[mb])
                        tk.op("dve", lambda: nc.vector.tensor_tensor(out=m_[:, 1, 0:tw], in0=ys[1][0][:, 0:tw], in1=g_[:, 1, 0:tw], op=ALU.mult),
                              reads=[ys[1][1], gb], writes=[mb])
                        tk.op("dve", lambda: nc.vector.tensor_tensor(out=m_[:, 0, 0:tw], in0=m_[:, 0, 0:tw], in1=m_[:, 1, 0:tw], op=ALU.add),
                              reads=[mb], writes=[mb])
                        tk.op("dve", lambda: nc.vector.tensor_tensor(out=m_[:, 1, 0:tw], in0=ys[2][0][:, 0:tw], in1=g_[:, 2, 0:tw], op=ALU.mult),
                              reads=[ys[2][1], gb, mb], writes=[mb])
                        s_, sb_ = stg.next()
                        tk.op("dve", lambda: nc.vector.tensor_tensor(out=s_[:, 0:tw], in0=m_[:, 0, 0:tw], in1=m_[:, 1, 0:tw], op=ALU.add),
                              reads=[mb], writes=[sb_])
                        tk.dma("act", [(MT[cc * 128:(cc + 1) * 128, t0:t0 + tw], s_[:, 0:tw])], reads=[sb_])
        tk.barrier()

    def phase_gemm_to_Y(l, Wd, ATd, K, lat_only):
        kc = K // 128
        kgs = [(k0, min(c.KG, kc - k0)) for k0 in range(0, kc, c.KG)]
        with ExitStack() as st:
            banks = fp_banks(st, 6)
            nbl = [dict(width=512, load=w_cols(Wd, b0, 512), subs=[(0, 512, b0)]) for b0 in range(0, D, 512)]
            if len(kgs) == 1:
                stg = Stager(st, 4, [128, 512], F32)

                def epi(pt, pb, m, n, info):
                    ti = info["tt"]
                    s_, sb_ = stg.next()
                    evac_copy(alt(), s_[:, 0:n], pt[:, 0:n], [pb], [sb_])
                    tk.dma("act", [(Y[ti * 128:(ti + 1) * 128, info["tag"]:info["tag"] + n], s_[:, 0:n])], reads=[sb_])
            else:
                ntt = (S if lat_only else T) // 128
                acc = sbt(st, [128, ntt, 512], F32, "acc")
                accb = [Buf() for _ in range(ntt)]

                def epi(pt, pb, m, n, info):
                    ti = info["tt"]
                    if info["kg"] == 0:
                        tk.op("act", lambda: nc.scalar.copy(out=acc[:, ti, 0:n], in_=pt[:, 0:n]), reads=[pb], writes=[accb[ti]])
                    else:
                        tk.op("dve", lambda: nc.vector.tensor_tensor(out=acc[:, ti, 0:n], in0=acc[:, ti, 0:n], in1=pt[:, 0:n], op=ALU.add),
                              reads=[pb, accb[ti]], writes=[accb[ti]])
                    if info["kg"] == info["nkg"] - 1:
                        tk.dma("act", [(Y[ti * 128:(ti + 1) * 128, info["tag"]:info["tag"] + n], acc[:, ti, 0:n])], reads=[accb[ti]])
            gemm("TM", kgs, a_from(ATd), c.tblocks(lat_only), nbl, epi, st, banks)
        tk.barrier()

    def phase_ff1(l, lat_only):
        with ExitStack() as st:
            banks = fp_banks(st, 6)
            rr = Rot([(sbt(st, [128, c.TB], F32, "r"), Buf()) for _ in range(3)])
            stg = Stager(st, 3, [128, c.TB], BF16)
            nbl = simple_nblocks(w_ff1[l], 0, c.DFF, lambda col: col)

            def epi(pt, pb, m, tw, info):
                r_, rb = rr.next()
                tk.op("act", lambda: nc.scalar.activation(out=r_[:, 0:tw], in_=pt[:, 0:tw], func=AF.Relu), reads=[pb], writes=[rb])
                s_, sb_ = stg.next()
                tk.op("dve", lambda: nc.vector.tensor_tensor(out=s_[:, 0:tw], in0=r_[:, 0:tw], in1=r_[:, 0:tw], op=ALU.mult),
                      reads=[rb], writes=[sb_])
                r0 = info["tag"]
                tk.dma("act", [(HID[r0:r0 + 128, info["t0"]:info["t0"] + tw], s_[:, 0:tw])], reads=[sb_])
            gemm("FM", [(0, DC)], a_from(HT), c.tblocks(lat_only), nbl, epi, st, banks)
        tk.barrier()

    gpm_prev = sbt(P, [128, 2, DC], F32, "gpm_prev")

    def run_layer(l):
        last = l == L - 1
        if l == 0:
            phase_vectors(l)
            if stop == "vectors": return True
            phase_modulate(l, gsa, 0)
        if stop == "mod1": return True
        phase_win(l)
        if stop == "win": return True
        with ExitStack() as lst:
            cs = conv_alloc(lst, last)
            gen = conv_mac_gen(l, last, cs)
            n_steps = c.H * sum((c.NTT if t0 < S else CTX // 128) for (t0, tw) in c.tblocks(last))
            n_ops = (c.CW // 128) * (2 + (1 if last else 2) * CONV_K) + 2
            rate = 1.15 * n_ops / n_steps
            credit = [0.0]

            def pump(drain=False):
                if drain:
                    for _ in gen:
                        pass
                    return
                credit[0] += rate
                while credit[0] >= 1.0:
                    credit[0] -= 1.0
                    try:
                        next(gen)
                    except StopIteration:
                        return
            phase_stats(l)
            if stop == "stats": return True
            phase_q(l, last)
            if stop == "q": return True
            phase_kv(l)
            if stop == "qkv": return True
            phase_attn(l, last, pump)
            if stop == "attn": return True
            phase_conv_ln(l, last, cs)
        if stop == "conv": return True
        phase_fourier(l, last)
        if stop == "fourier": return True
        phase_merge(l, last)
        if stop == "merge": return True
        phase_gemm_to_Y(l, w_out[l], MT, D, last)
        if stop == "wout": return True
        phase_modulate(l, gsm, 3, lat_only=last, resid=gpa)
        if stop == "mod2": return True
        phase_ff1(l, last)
        if stop == "ff1": return True
        phase_gemm_to_Y(l, w_ff2[l], HID, c.DFF, last)
        if stop == "ff2": return True
        if last:
            phase_modulate(l, None, 0, lat_only=True, resid=gpm)
        else:
            tk.op("dve", lambda: nc.vector.tensor_copy(out=gpm_prev[:], in_=gpm[:]), reads=[Bmod], writes=[Bmod])
            tk.barrier()
            phase_vectors(l + 1)
            phase_modulate(l + 1, gsa, 0, lat_only=False, resid=gpm_prev)
        return False

    for l in range(L):
        if run_layer(l):
            break
    rows = 256
    tk.dma("sp", [(y_out[r0:r0 + rows, :], R[r0:r0 + rows, :]) for r0 in range(0, S, rows)])
    tk.barrier()
    return nc, tk


def make_consts(cfg):
    c = cfg
    S, CTX, T = c.S, c.CTX, c.T
    out = {}
    out["ident"] = np.eye(128, dtype=np.float32)
    perm = np.zeros((128, 128), np.float32)
    for m in range(128):
        blk, mm_ = divmod(m, 64)
        a, r = divmod(mm_, 32)
        hf, f = divmod(r, 16)
        perm[blk * 64 + a * 32 + (1 - hf) * 16 + f, m] = 1.0
    out["perm"] = perm
    nf = 16
    inv_freq = np.power(np.float32(ROPE_THETA), -np.arange(nf, dtype=np.float32) / np.float32(nf)).astype(np.float32)
    pos = np.stack([np.repeat(np.arange(S // GRID_W, dtype=np.float32), GRID_W),
                    np.tile(np.arange(GRID_W, dtype=np.float32), S // GRID_W)], -1)
    ang = (pos[:, :, None] * inv_freq).astype(np.float32)
    cosv, sinv = np.cos(ang), np.sin(ang)
    rope = np.zeros((2, 128, T), np.float32)
    rope[0, :, S:] = 1.0
    for blk in range(2):
        for a in range(2):
            for hf in range(2):
                rows = slice(blk * 64 + a * 32 + hf * 16, blk * 64 + a * 32 + hf * 16 + 16)
                rope[0, rows, :S] = cosv[:, a, :].T
                rope[1, rows, :S] = (sinv[:, a, :].T) * (-1.0 if hf == 0 else 1.0)
    out["rope"] = rope

    def dft(n):
        k = np.arange(n, dtype=np.float64)
        a = 2.0 * np.pi * np.outer(k, k) / n
        return np.cos(a) / np.sqrt(n), np.sin(a) / np.sqrt(n)
    cc, sc = dft(c.FGW)
    dftc = np.zeros((c.FW, 2 * c.FW), np.float32)
    for g in range(4):
        sl = slice(g * c.FGW, (g + 1) * c.FGW)
        dftc[sl, sl] = cc
        dftc[sl, c.FW + g * c.FGW:c.FW + (g + 1) * c.FGW] = sc
    out["dftc"] = dftc
    cl, sl_ = dft(S)
    out["dftp"] = np.concatenate([cl, -sl_], 0).astype(np.float32)
    cx, sx = dft(CTX)
    out["dftx"] = np.concatenate([cx, -sx], 0).astype(np.float32)
    return out


def pack_vecs(cfg, p):
    c = cfg
    rows = []
    for l in range(c.DEPTH):
        parts = [p[n][l].reshape(-1, 128) for n in ("g_mix_pre", "g_mix_post", "g_mlp_pre", "g_mlp_post", "b_mod",
                                                     "conv_b", "conv_ln_g", "conv_ln_b", "q_norm_g", "kv_norm_g", "b_gate")]
        parts.append(p["conv_w"][l].reshape(-1, 128))
        rows.append(np.concatenate(parts, 0))
    v = np.ascontiguousarray(np.stack(rows, 0), dtype=np.float32)
    assert v.shape == (c.DEPTH, c.NV, 128), v.shape
    return v


WEIGHTS = ("w_mod_a", "w_mod_b", "w_in", "w_uq", "w_ukv", "w_pf", "w_pc", "w_pm", "w_gate_a", "w_gate_b",
           "w_out", "w_ff1", "w_ff2")


def make_in_maps(cfg, inputs):
    c = cfg
    B = inputs["x"].shape[0]
    consts = make_consts(c)
    vecs = pack_vecs(c, inputs)
    shared = {k: np.ascontiguousarray(inputs[k], dtype=np.float32) for k in WEIGHTS}
    shared.update(consts)
    shared["vecs"] = vecs
    maps = []
    for b in range(B):
        m = dict(shared)
        m["xin"] = np.ascontiguousarray(np.concatenate([inputs["x"][b], inputs["ctx"][b]], 0), dtype=np.float32)
        m["cond"] = np.ascontiguousarray(np.stack([inputs["c"][b], inputs["c_ctx"]], 0).reshape(2 * c.DC, 128), dtype=np.float32)
        maps.append(m)
    return maps


def kernel(**inputs):
    cfg = Cfg()
    nc, tk = build_program(cfg)
    maps = make_in_maps(cfg, inputs)
    res = run_bass_kernel_spmd(nc, maps, core_ids=list(range(len(maps))))
    return np.stack([r["y"] for r in res.results], 0).astype(np.float32)
```
